# Optimizing a Trainium2 kernel written in Bass

```python
import jax, jax.numpy as jnp
from jax import lax
import numpy as np

D_MODEL = 1024
BATCH = 16
SEQ = 2048
DEPTH = 1

PLE_DIM = 256
CONV_WIDTH = D_MODEL // 2
CONV_GROUPS = 8
CONV_K = 3
RET_HEADS = 4
RET_HEAD_DIM = 128
RET_WIDTH = RET_HEADS * RET_HEAD_DIM
MIX_WIDTH = CONV_WIDTH + RET_WIDTH
IN_COLS = 3 * CONV_WIDTH + 4 * RET_WIDTH
D_FF = -(-8 * D_MODEL // (3 * 256)) * 256
CHUNK = 128
ROPE_BASE = 10000.0
EPS = 1e-6

kernel_name = "hybrid_shortconv_retention_block"


def rmsnorm(x, g):
    xf = x.astype(jnp.float32)
    y = xf * lax.rsqrt(jnp.mean(xf * xf, axis=-1, keepdims=True) + EPS)
    return (y * g.astype(jnp.float32)).astype(x.dtype)


def head_groupnorm(y, g):
    mu = jnp.mean(y, axis=-1, keepdims=True)
    var = jnp.mean(jnp.square(y - mu), axis=-1, keepdims=True)
    yn = (y - mu) * lax.rsqrt(var + EPS)
    b, s, h, d = y.shape
    return yn.reshape(b, s, h * d) * g.astype(jnp.float32)


def rope(x, pos):
    half = x.shape[-1] // 2
    inv_freq = ROPE_BASE ** (-jnp.arange(half, dtype=jnp.float32) / half)
    ang = pos[:, None] * inv_freq[None, :]
    cos = jnp.cos(ang)[None, :, None, :]
    sin = jnp.sin(ang)[None, :, None, :]
    x1, x2 = x[..., :half], x[..., half:]
    return jnp.concatenate([x1 * cos - x2 * sin, x2 * cos + x1 * sin], axis=-1)


def retention_chunkwise(q, k, v):
    b, s, h, d = q.shape
    n = s // CHUNK
    log_gamma = jnp.log(1.0 - jnp.power(2.0, -5.0 - jnp.arange(h, dtype=jnp.float32)))

    def to_chunks(t):
        return t.reshape(b, n, CHUNK, h, d).transpose(0, 3, 1, 2, 4)

    qc, kc, vc = to_chunks(q), to_chunks(k), to_chunks(v)
    idx = jnp.arange(CHUNK, dtype=jnp.float32)
    diff = idx[:, None] - idx[None, :]
    decay_mask = jnp.where(diff[None] >= 0,
                           jnp.exp(log_gamma[:, None, None] * jnp.maximum(diff, 0.0)[None]),
                           0.0)

    scores = jnp.einsum('bhncd,bhnmd->bhncm', qc, kc) * decay_mask[None, :, None]
    intra = jnp.einsum('bhncm,bhnme->bhnce', scores, vc)

    k_decay = jnp.exp(log_gamma[:, None] * (CHUNK - 1 - idx)[None])
    kv = jnp.einsum('bhncd,hc,bhnce->bhnde', kc, k_decay, vc)
    chunk_decay = jnp.exp(log_gamma * CHUNK)[None, :, None, None]

    def step(state, kv_c):
        return state * chunk_decay + kv_c, state

    init = jnp.zeros((b, h, d, d), jnp.float32)
    _, states_in = lax.scan(step, init, jnp.moveaxis(kv, 2, 0))
    states_in = jnp.moveaxis(states_in, 0, 2)

    q_decay = jnp.exp(log_gamma[:, None] * (idx + 1.0)[None])
    cross = jnp.einsum('bhncd,hc,bhnde->bhnce', qc, q_decay, states_in)

    out = intra + cross
    return out.transpose(0, 2, 3, 1, 4).reshape(b, s, h, d)


def setup_inputs(seed: int = 0) -> dict:
    key = jax.random.key(seed)
    ks = jax.random.split(key, 16)
    f32 = jnp.float32

    def w(k, shape, fan_in):
        return jax.random.normal(k, shape, f32) * (fan_in ** -0.5)

    def gain(k, shape):
        return 1.0 + 0.02 * jax.random.normal(k, shape, f32)

    return {
        "x": jax.random.normal(ks[0], (BATCH, SEQ, D_MODEL), f32),
        "p": jax.random.normal(ks[1], (DEPTH, BATCH, SEQ, PLE_DIM), f32),
        "g_mix": gain(ks[2], (DEPTH, D_MODEL)),
        "w_in": w(ks[3], (DEPTH, D_MODEL, IN_COLS), D_MODEL),
        "conv_w": w(ks[4], (DEPTH, CONV_K, CONV_WIDTH), CONV_K),
        "ret_gn": gain(ks[5], (DEPTH, RET_WIDTH)),
        "w_out": w(ks[6], (DEPTH, MIX_WIDTH, D_MODEL), MIX_WIDTH),
        "g_ffn": gain(ks[7], (DEPTH, D_MODEL)),
        "w_gate": w(ks[8], (DEPTH, D_MODEL, D_FF), D_MODEL),
        "w_up": w(ks[9], (DEPTH, D_MODEL, D_FF), D_MODEL),
        "w_down": w(ks[10], (DEPTH, D_FF, D_MODEL), D_FF),
        "g_ple": gain(ks[11], (DEPTH, D_MODEL)),
        "w_ple_gate": w(ks[12], (DEPTH, D_MODEL, D_MODEL), D_MODEL),
        "w_ple_proj": w(ks[13], (DEPTH, PLE_DIM, D_MODEL), PLE_DIM),
        "g_final": gain(ks[14], (D_MODEL,)),
    }


def reference(x, p, g_mix, w_in, conv_w, ret_gn, w_out, g_ffn, w_gate, w_up,
              w_down, g_ple, w_ple_gate, w_ple_proj, g_final):
    b, s, _ = x.shape
    pos = jnp.arange(s, dtype=jnp.float32)
    splits = np.cumsum([CONV_WIDTH] * 3 + [RET_WIDTH] * 3).tolist()
    h = x
    for i in range(DEPTH):
        u = rmsnorm(h, g_mix[i])
        proj = u @ w_in[i]
        cb, cc, cx, rq, rk, rv, rg = jnp.split(proj, splits, axis=-1)

        z = cc * cx
        z_pad = jnp.pad(z, ((0, 0), (CONV_K - 1, 0), (0, 0)))
        conv = sum(conv_w[i][j] * z_pad[:, j:j + s] for j in range(CONV_K))
        y_conv = cb * conv

        q = rope(rq.reshape(b, s, RET_HEADS, RET_HEAD_DIM).astype(jnp.float32), pos)
        k = rope(rk.reshape(b, s, RET_HEADS, RET_HEAD_DIM).astype(jnp.float32), pos)
        k = k * (RET_HEAD_DIM ** -0.5)
        v = rv.reshape(b, s, RET_HEADS, RET_HEAD_DIM).astype(jnp.float32)
        y_ret = head_groupnorm(retention_chunkwise(q, k, v), ret_gn[i])
        y_ret = (jax.nn.silu(rg.astype(jnp.float32)) * y_ret).astype(x.dtype)

        h = h + jnp.concatenate([y_conv, y_ret], axis=-1) @ w_out[i]

        u = rmsnorm(h, g_ffn[i])
        h = h + (jax.nn.silu(u @ w_gate[i]) * (u @ w_up[i])) @ w_down[i]

        u = rmsnorm(h, g_ple[i])
        h = h + jax.nn.sigmoid(u @ w_ple_gate[i]) * (p[i] @ w_ple_proj[i])
    return rmsnorm(h, g_final)
```

```python
import numpy as np
import ml_dtypes
import concourse.bass as bass
import concourse.mybir as mybir
from concourse.bass_utils import run_bass_kernel_spmd

F32 = mybir.dt.float32
BF16 = mybir.dt.bfloat16
AF = mybir.ActivationFunctionType
ALU = mybir.AluOpType
AX = mybir.AxisListType

D = 1024
SEQ = 2048
NCORE = 8
TOKC = 4096
GT = 1024
NCH = 8
NG = TOKC // GT
INC = 3584
DFF = 2816
NFF = 22
PLE = 256
EPS = 1e-6
HEADS = 4
FFB = 4
ARENA_BYTES = 206 * 1024


class Op:
    __slots__ = ("eng", "fn", "deps", "is_dma", "key", "signal", "tick", "idx")


class Prog:
    def __init__(self):
        self.ops = []
        self.recs = {}
        self.dma_count = {}

    def _access(self, reg, idx, is_write, out):
        space, lo, hi = reg
        lst = self.recs.setdefault(space, [])
        me = self.ops_eng(idx)
        keep = []
        excl = space == "ps"
        for rec in lst:
            rlo, rhi, ridx, rw = rec
            if rhi <= lo or rlo >= hi:
                keep.append(rec)
                continue
            if ridx != idx and (is_write or rw or (excl and self.ops_eng(ridx) != me)):
                kind = "RAW" if (rw and not is_write) else "X"
                out.append((ridx, kind))
            if is_write and rlo >= lo and rhi <= hi:
                continue
            if (not is_write) and (not rw) and rlo == lo and rhi == hi and self.ops_eng(ridx) == me:
                continue
            keep.append(rec)
        if (not is_write) and space == "sb" and not any(rec[3] and rec[0] < hi and rec[1] > lo for rec in lst):
            print("WARNING: read of never-written region", reg, "op", idx, self._cur_eng)
        keep.append([lo, hi, idx, is_write])
        self.recs[space] = keep

    def ops_eng(self, idx):
        if idx == len(self.ops):
            return self._cur_eng
        o = self.ops[idx]
        return ("dma", o.idx) if o.is_dma else o.eng

    def add(self, eng, fn, r=(), w=(), dma_key=None):
        op = Op()
        op.eng = eng
        op.fn = fn
        op.is_dma = dma_key is not None
        op.key = dma_key
        op.signal = op.is_dma
        op.tick = 0
        op.idx = len(self.ops)
        self._cur_eng = ("dma", op.idx) if op.is_dma else eng
        raw = []
        for reg in r:
            self._access(reg, op.idx, False, raw)
        for reg in w:
            self._access(reg, op.idx, True, raw)
        deps = {}
        for (pidx, kind) in raw:
            p = self.ops[pidx]
            if p.is_dma:
                deps[pidx] = True
                continue
            if (not op.is_dma) and p.eng == eng:
                if eng == "pe" or kind != "RAW":
                    continue
            deps[pidx] = True
        best = {}
        final = []
        for pidx in deps:
            p = self.ops[pidx]
            if p.is_dma:
                final.append(pidx)
            else:
                if p.eng not in best or best[p.eng] < pidx:
                    best[p.eng] = pidx
        final.extend(best.values())
        for pidx in final:
            self.ops[pidx].signal = True
        op.deps = final
        self.ops.append(op)
        return op

    def emit(self, nc, block, sems, dma_sems):
        cnt = {}
        for op in self.ops:
            if op.is_dma:
                c = self.dma_count.get(op.key, 0) + 16
                self.dma_count[op.key] = c
                op.tick = c
            elif op.signal:
                c = cnt.get(op.eng, 0) + 1
                cnt[op.eng] = c
                op.tick = c
        self.final_cnt = cnt
        ops = self.ops

        def run(engname, e):
            waited = {}
            for op in ops:
                if op.eng != engname:
                    continue
                for pidx in op.deps:
                    p = ops[pidx]
                    if p.is_dma:
                        sem = dma_sems[p.key]
                        sk = ("d", p.key)
                    else:
                        sem = sems[p.eng]
                        sk = ("e", p.eng)
                    if waited.get(sk, 0) >= p.tick:
                        continue
                    waited[sk] = p.tick
                    e.wait_ge(sem, p.tick)
                ins = op.fn(e)
                if op.is_dma:
                    ins.then_inc(dma_sems[op.key], 16)
                elif op.signal:
                    ins.then_inc(sems[op.eng], 1)
            if engname == "sp":
                for k, c in self.dma_count.items():
                    if waited.get(("d", k), 0) < c:
                        e.wait_ge(dma_sems[k], c)

        @block.sync
        def _(e):
            run("sp", e)

        @block.scalar
        def _(e):
            run("act", e)

        @block.gpsimd
        def _(e):
            run("pool", e)

        @block.tensor
        def _(e):
            run("pe", e)

        @block.vector
        def _(e):
            run("dve", e)


class Buf:
    def __init__(self, arena, lo, n):
        self.arena = arena
        self.lo = lo
        self.n = n

    def f32(self, pat=None, **kw):
        a = self.arena[:, self.lo // 4:(self.lo + self.n) // 4]
        return a.rearrange(pat, **kw) if pat else a

    def bf(self, pat=None, **kw):
        a = self.arena[:, self.lo // 4:(self.lo + self.n) // 4].bitcast(BF16)
        return a.rearrange(pat, **kw) if pat else a

    def r(self, lo=0, hi=None):
        return ("sb", self.lo + lo, self.lo + (self.n if hi is None else hi))


class Bump:
    def __init__(self, arena, lo, hi):
        self.arena, self.lo, self.hi, self.p = arena, lo, hi, lo

    def alloc(self, n):
        n = (n + 63) // 64 * 64
        b = Buf(self.arena, self.p, n)
        self.p += n
        assert self.p <= self.hi, ("arena overflow", self.p, self.hi)
        return b


class Ring:
    def __init__(self, arena, lo, size):
        self.arena, self.lo, self.size = arena, lo, size
        self.head = 0
        self.live = []

    def try_alloc(self, n):
        start = self.head if self.head + n <= self.size else 0
        if all(start + n <= o or start >= o + m for (o, m) in self.live):
            self.head = start + n
            self.live.append((start, n))
            return Buf(self.arena, self.lo + start, n)
        return None

    def free(self, buf):
        self.live.remove((buf.lo - self.lo, buf.n))


def build_nc(ngroups=NG, stop_phase=3):
    nc = bass.Bass("TRN2", target_bir_lowering=False)
    dt_in = lambda name, shape, dt=F32: nc.dram_tensor(name, shape, dt, kind="ExternalInput").ap()
    x_d = dt_in("x", [TOKC, D])
    p_d = dt_in("p", [TOKC, PLE])
    w_in_d = dt_in("w_in", [D, INC])
    w_out_d = dt_in("w_out", [D, D])
    w_gate_d = dt_in("w_gate", [D, DFF])
    w_up_d = dt_in("w_up", [D, DFF])
    w_down_d = dt_in("w_down", [DFF, D])
    w_pg_d = dt_in("w_ple_gate", [D, D])
    w_pp_d = dt_in("w_ple_proj", [PLE, D])
    g_mix_d = dt_in("g_mix", [D])
    g_ffn_d = dt_in("g_ffn", [D])
    g_ple_d = dt_in("g_ple", [D])
    g_fin_d = dt_in("g_final", [D])
    retgn_d = dt_in("ret_gn", [512])
    convw_d = dt_in("convw", [128, 12])
    ident_d = dt_in("ident", [128, 128], BF16)
    cos_d = dt_in("cos_t", [128, 16 * 64])
    sin_d = dt_in("sin_t", [128, 16 * 128])
    mask_d = dt_in("maskT", [128, 512])
    kdec_d = dt_in("kdec", [128, 512])
    qdec_d = dt_in("qdec", [128, 512])
    out_d = nc.dram_tensor("out", [TOKC, D], F32, kind="ExternalOutput").ap()

    log_gamma = np.log(1.0 - np.power(2.0, -5.0 - np.arange(HEADS, dtype=np.float64)))
    cdec = [float(np.exp(lg * 128.0)) for lg in log_gamma]

    P = Prog()
    from contextlib import ExitStack
    es = ExitStack()
    arena_t = es.enter_context(nc.sbuf_tensor("arena", [128, ARENA_BYTES // 4], F32))
    ps_t = es.enter_context(nc.psum_tensor("ps", [128, 8, 512], F32))
    arena = arena_t[:]
    pers = Bump(arena, 0, ARENA_BYTES)

    hbuf = [pers.alloc(4096) for _ in range(NCH)]
    cosb = pers.alloc(NCH * 64 * 4)
    sinb = pers.alloc(NCH * 128 * 4)
    maskb = pers.alloc(2048)
    kdecb = pers.alloc(2048)
    qdecb = pers.alloc(2048)
    retgnb = pers.alloc(2048)
    identb = pers.alloc(256)
    stateb = pers.alloc(2048)
    statebf = pers.alloc(1024)
    convwb = pers.alloc(64)
    halob = pers.alloc(64)
    statb = pers.alloc(256)
    epsb = pers.alloc(64)
    WORK_LO = pers.p
    WORK_SIZE = 71 * 1024
    RING_LO = WORK_LO + WORK_SIZE
    RING_SIZE = ARENA_BYTES - RING_LO
    ring = Ring(arena, RING_LO, RING_SIZE)
    ident = identb.bf()

    ps_ctr = [0]

    def nb():
        b = ps_ctr[0] % 8
        ps_ctr[0] += 1
        return b

    def psr(b):
        return ("ps", b, b + 1)

    def psf(b):
        return ps_t[:, b, :]

    def psb(b):
        return ps_t[:, b, :].bitcast(BF16)

    dma_keys = set()

    def dma(eng, out, in_, r, w, key):
        dma_keys.add(key)
        P.add(eng, lambda e: e.dma_start(out=out, in_=in_), r=r, w=w, dma_key=key)

    def mm(out, lhsT, rhs, start, stop, r, w):
        P.add("pe", lambda e: e.matmul(out, lhsT=lhsT, rhs=rhs, start=start, stop=stop), r=r, w=w)

    def tr(out, in_, r, w):
        P.add("pe", lambda e: e.transpose(out=out, in_=in_, identity=ident), r=list(r) + [identb.r()], w=w)

    def tt(out, in0, in1, op, r, w):
        P.add("dve", lambda e: e.tensor_tensor(out=out, in0=in0, in1=in1, op=op), r=r, w=w)

    def stt(out, in0, scalar, in1, op0, op1, r, w):
        P.add("dve", lambda e: e.scalar_tensor_tensor(out=out, in0=in0, scalar=scalar, in1=in1, op0=op0, op1=op1), r=r, w=w)

    def ts(out, in0, s1, s2, op0, op1, r, w):
        if op1 is None:
            P.add("dve", lambda e: e.tensor_scalar(out=out, in0=in0, scalar1=s1, scalar2=None, op0=op0), r=r, w=w)
        else:
            P.add("dve", lambda e: e.tensor_scalar(out=out, in0=in0, scalar1=s1, scalar2=s2, op0=op0, op1=op1), r=r, w=w)

    def rsum(out, in_, r, w):
        P.add("dve", lambda e: e.reduce_sum(out=out, in_=in_, axis=AX.X), r=r, w=w)

    def mset(ap, val, w):
        P.add("dve", lambda e: e.memset(ap, val), r=[], w=w)

    def act(out, in_, func, r, w, accum_out=None):
        if accum_out is None:
            P.add("act", lambda e: e.activation(out=out, in_=in_, func=func), r=r, w=w)
        else:
            P.add("act", lambda e: e.activation(out=out, in_=in_, func=func, accum_out=accum_out), r=r, w=w)

    cidx = [0]

    def load_const(buf, src, bf=False):
        dst = buf.bf() if bf else buf.f32()
        dst = dst[:, 0:src.shape[1]]
        dma("sp", dst, src, [], [buf.r()], "c%d" % cidx[0])
        cidx[0] += 1

    P.add("dve", lambda e: e.memset(epsb.f32(), EPS), r=[], w=[epsb.r()])
    load_const(identb, ident_d, bf=True)
    load_const(maskb, mask_d)
    load_const(kdecb, kdec_d)
    load_const(qdecb, qdec_d)
    load_const(convwb, convw_d)
    load_const(retgnb, retgn_d.partition_broadcast(128))

    def wview(d, c0, c1):
        return d.rearrange("(kc p) n -> p kc n", p=128)[:, :, c0:c1]

    def piece_list(g):
        L = []
        for nm, c0 in (("q", 1536), ("k", 2048), ("v", 2560), ("g", 3072), ("cc", 512), ("cx", 1024), ("cb", 0)):
            L.append(((g, nm), wview(w_in_d, c0, c0 + 512), 8 * 512 * 2))
        for hf in range(2):
            L.append(((g, "wo%d" % hf), wview(w_out_d, hf * 512, hf * 512 + 512), 8 * 512 * 2))
        if stop_phase >= 2:
            f0 = 0
            bi = 0
            while f0 < NFF:
                nf = min(FFB, NFF - f0)
                L.append(((g, "fg%d" % bi), wview(w_gate_d, f0 * 128, (f0 + nf) * 128), 8 * nf * 128 * 2))
                L.append(((g, "fu%d" % bi), wview(w_up_d, f0 * 128, (f0 + nf) * 128), 8 * nf * 128 * 2))
                L.append(((g, "fd%d" % bi), w_down_d.rearrange("(f p) n -> p f n", p=128)[:, f0:f0 + nf, :], nf * 1024 * 2))
                f0 += nf
                bi += 1
        if stop_phase >= 3:
            for hf in range(2):
                L.append(((g, "pg%d" % hf), wview(w_pg_d, hf * 512, hf * 512 + 512), 8 * 512 * 2))
            L.append(((g, "pp"), w_pp_d.rearrange("(k p) n -> p k n", p=128), 2 * 1024 * 2))
        return L

    pending = []
    for g in range(ngroups):
        pending.extend(piece_list(g))
    pending.reverse()
    loaded = {}
    wsem_ctr = [0]
    NWSEM = 24

    def pump():
        while pending:
            name, src, nbytes = pending[-1]
            b = ring.try_alloc(nbytes)
            if b is None:
                return
            pending.pop()
            loaded[name] = b
            dst = b.bf("p (a n) -> p a n", a=src.shape[1])
            key = "w%d" % (wsem_ctr[0] % NWSEM)
            wsem_ctr[0] += 1
            dma("pool", dst, src, [], [b.r()], key)

    def W(g, nm):
        pump()
        assert (g, nm) in loaded, ("weight piece not resident (ring too small?)", g, nm, ring.live)
        return loaded[(g, nm)]

    def wfree(g, nm):
        ring.free(loaded.pop((g, nm)))
        pump()

    stat = statb.f32()

    def st(i, n=1):
        return stat[:, i:i + n]

    def str_(i, n=1):
        return statb.r(i * 4, (i + n) * 4)

    def rstd_from_ss(ss_i, rs_i, n, w=1):
        sq = st(rs_i, w)
        P.add("act", lambda e: e.activation(out=sq, in_=st(ss_i, w), func=AF.Sqrt, bias=epsb.f32()[:, 0:1], scale=1.0 / n),
              r=[str_(ss_i, w), epsb.r()], w=[str_(rs_i, w)])
        P.add("dve", lambda e: e.reciprocal(out=sq, in_=sq), r=[str_(rs_i, w)], w=[str_(rs_i, w)])

    def norm_to_bf16(c, gbuf, ubuf, slot):
        h = hbuf[c].f32()
        ss_i, rs_i = slot * 2, slot * 2 + 1
        act(ubuf.bf(), h, AF.Square, [hbuf[c].r()], [ubuf.r(), str_(ss_i)], accum_out=st(ss_i))
        rstd_from_ss(ss_i, rs_i, D)
        stt(ubuf.bf(), h, st(rs_i), gbuf.f32(), ALU.mult, ALU.mult, [hbuf[c].r(), str_(rs_i), gbuf.r()], [ubuf.r()])

    def transpose_to(ubuf, nk, dst_ap, dst_regs):
        b = nb()
        u = ubuf.bf()
        for k in range(nk):
            tr(psb(b)[:, k * 128:(k + 1) * 128], u[:, k * 128:(k + 1) * 128], [ubuf.r(k * 256, (k + 1) * 256)], [psr(b)])
        src = psb(b)[:, 0:nk * 128].rearrange("p (k n) -> p k n", k=nk)
        act(dst_ap, src, AF.Copy, [psr(b)], dst_regs)

    def load_g(buf, src, key):
        dma("sp", buf.f32(), src.partition_broadcast(128), [], [buf.r()], key)

    for g in range(ngroups):
        seq_first = (g % 2 == 0)
        tok0 = g * GT
        pos_chunk0 = (g % 2) * NCH

        dma("sp", cosb.f32(), cos_d[:, pos_chunk0 * 64:(pos_chunk0 + NCH) * 64], [], [cosb.r()], "tabc")
        dma("sp", sinb.f32(), sin_d[:, pos_chunk0 * 128:(pos_chunk0 + NCH) * 128], [], [sinb.r()], "tabs")

        if seq_first:
            mset(stateb.f32(), 0.0, [stateb.r()])
            mset(statebf.bf(), 0.0, [statebf.r()])
            mset(halob.f32(), 0.0, [halob.r()])

        wk = Bump(arena, WORK_LO, WORK_LO + WORK_SIZE)
        gmixb = wk.alloc(4096)
        ub = [wk.alloc(2048) for _ in range(2)]
        uT = [wk.alloc(8192) for _ in range(2)]
        yT = [wk.alloc(8192) for _ in range(2)]
        ropeA = wk.alloc(2048)
        ropeB = wk.alloc(2048)
        qksb = wk.alloc(2048)
        qkT = [wk.alloc(2048) for _ in range(2)]
        vsb = [wk.alloc(1024) for _ in range(2)]
        gateb = [wk.alloc(2048) for _ in range(2)]
        ksb = wk.alloc(1024)
        scsb = [wk.alloc(1024) for _ in range(2)]
        y1b = wk.alloc(2048)
        yb = wk.alloc(2048)
        yrb = wk.alloc(1024)
        ccb = wk.alloc(2048)
        zb = wk.alloc(514 * 4)
        cvb = wk.alloc(2048)
        load_g(gmixb, g_mix_d, "g0")

        def uT4(t_):
            return uT[t_ % 2].bf("p (c k n) -> p c k n", c=4, k=8)

        def yT4(t_):
            return yT[t_ % 2].bf("p (c m n) -> p c m n", c=4, m=8)

        def S1(c):
            dma("sp", hbuf[c].f32(), x_d[tok0 + c * 128: tok0 + (c + 1) * 128, :], [], [hbuf[c].r()], "h%d" % c)
            norm_to_bf16(c, gmixb, ub[c % 2], c % 2)

        def S2(c):
            t_, c4 = divmod(c, 4)
            transpose_to(ub[c % 2], 8, uT4(t_)[:, c4], [uT[t_ % 2].r(c4 * 2048, (c4 + 1) * 2048)])

        def mm_tok(c, wbuf, bank):
            t_, c4 = divmod(c, 4)
            wv = wbuf.bf("p (k n) -> p k n", k=8)
            ur = uT[t_ % 2].r(c4 * 2048, (c4 + 1) * 2048)
            for kc in range(8):
                mm(psf(bank), uT4(t_)[:, c4, kc, :], wv[:, kc, :], kc == 0, kc == 7, [ur, wbuf.r()], [psr(bank)])

        def S3(c):
            bq, bk, bv, bg = nb(), nb(), nb(), nb()
            mm_tok(c, W(g, "q"), bq)
            mm_tok(c, W(g, "k"), bk)
            mm_tok(c, W(g, "v"), bv)
            mm_tok(c, W(g, "g"), bg)
            cosv = cosb.f32("p (c d) -> p c d", c=NCH)[:, c, :].unsqueeze(1).unsqueeze(1).broadcast_to([128, 4, 2, 64])
            sinv = sinb.f32("p (c t d) -> p c t d", c=NCH, t=2)
            a4 = ropeA.f32("p (h t d) -> p h t d", h=4, t=2)
            b4 = ropeB.f32("p (h t d) -> p h t d", h=4, t=2)
            for (bank, off) in ((bq, 0), (bk, 512)):
                x4 = psf(bank).rearrange("p (h t d) -> p h t d", h=4, t=2)
                tt(a4, x4, cosv, ALU.mult, [psr(bank), cosb.r()], [ropeA.r()])
                for t2 in range(2):
                    sv = sinv[:, c, t2, :].unsqueeze(1).broadcast_to([128, 4, 64])
                    tt(b4[:, :, t2, :], x4[:, :, 1 - t2, :], sv, ALU.mult, [psr(bank), sinb.r()], [ropeB.r()])
                tt(qksb.bf()[:, off:off + 512], ropeA.f32(), ropeB.f32(), ALU.add, [ropeA.r(), ropeB.r()],
                   [qksb.r(off * 2, (off + 512) * 2)])
            tt(ksb.bf(), qksb.bf()[:, 512:1024], kdecb.f32(), ALU.mult, [qksb.r(1024, 2048), kdecb.r()], [ksb.r()])
            act(vsb[c % 2].bf(), psf(bv), AF.Copy, [psr(bv)], [vsb[c % 2].r()])
            act(gateb[c % 2].f32(), psf(bg), AF.Silu, [psr(bg)], [gateb[c % 2].r()])
            tt(gateb[c % 2].f32(), gateb[c % 2].f32(), retgnb.f32(), ALU.mult, [gateb[c % 2].r(), retgnb.r()], [gateb[c % 2].r()])

        def S4(c):
            transpose_to(qksb, 8, qkT[c % 2].bf("p (k n) -> p k n", k=8), [qkT[c % 2].r()])

        def S5(c):
            b = nb()
            qk = qkT[c % 2].bf("p (k n) -> p k n", k=8)
            for h in range(HEADS):
                mm(psf(b)[:, h * 128:(h + 1) * 128], qk[:, 4 + h, :], qk[:, h, :], True, True, [qkT[c % 2].r()], [psr(b)])
            tt(scsb[c % 2].bf(), psf(b), maskb.f32(), ALU.mult, [psr(b), maskb.r()], [scsb[c % 2].r()])

        def S6(c):
            boi, boc, bkv = nb(), nb(), nb()
            qk = qkT[c % 2].bf("p (k n) -> p k n", k=8)
            sc = scsb[c % 2].bf("p (h n) -> p h n", h=4)
            v = vsb[c % 2].bf()
            for h in range(HEADS):
                sl = slice(h * 128, (h + 1) * 128)
                mm(psf(boi)[:, sl], sc[:, h, :], v[:, sl], True, True, [scsb[c % 2].r(), vsb[c % 2].r()], [psr(boi)])
            for h in range(HEADS):
                sl = slice(h * 128, (h + 1) * 128)
                mm(psf(boc)[:, sl], qk[:, h, :], statebf.bf()[:, sl], True, True, [qkT[c % 2].r(), statebf.r()], [psr(boc)])
            for h in range(HEADS):
                sl = slice(h * 128, (h + 1) * 128)
                mm(psf(bkv)[:, sl], ksb.bf()[:, sl], v[:, sl], True, True, [ksb.r(), vsb[c % 2].r()], [psr(bkv)])
            tt(y1b.f32(), psf(boc), qdecb.f32(), ALU.mult, [psr(boc), qdecb.r()], [y1b.r()])
            tt(yb.f32(), psf(boi), y1b.f32(), ALU.add, [psr(boi), y1b.r()], [yb.r()])
            for h in range(HEADS):
                sl = slice(h * 128, (h + 1) * 128)
                sr = stateb.r(h * 512, (h + 1) * 512)
                stt(stateb.f32()[:, sl], stateb.f32()[:, sl], cdec[h], psf(bkv)[:, sl], ALU.mult, ALU.add, [sr, psr(bkv)], [sr])
            act(statebf.bf(), stateb.f32(), AF.Copy, [stateb.r()], [statebf.r()])
            y3 = yb.f32("p (h d) -> p h d", h=4)
            B0 = 16
            act(y1b.f32(), yb.f32(), AF.Square, [yb.r()], [y1b.r()])
            rsum(st(B0, 4), y3, [yb.r()], [str_(B0, 4)])
            rsum(st(B0 + 4, 4), y1b.f32("p (h d) -> p h d", h=4), [y1b.r()], [str_(B0 + 4, 4)])
            ts(st(B0 + 8, 4), st(B0, 4), 1.0 / 128, None, ALU.mult, None, [str_(B0, 4)], [str_(B0 + 8, 4)])
            tt(st(B0 + 12, 4), st(B0 + 8, 4), st(B0 + 8, 4), ALU.mult, [str_(B0 + 8, 4)], [str_(B0 + 12, 4)])
            stt(st(B0 + 16, 4), st(B0 + 4, 4), 1.0 / 128, st(B0 + 12, 4), ALU.mult, ALU.subtract,
                [str_(B0 + 4, 4), str_(B0 + 12, 4)], [str_(B0 + 16, 4)])
            rstd_from_ss(B0 + 16, B0 + 20, 1.0, w=4)
            mean_b = st(B0 + 8, 4).unsqueeze(2).broadcast_to([128, 4, 128])
            rstd_b = st(B0 + 20, 4).unsqueeze(2).broadcast_to([128, 4, 128])
            tt(y3, y3, mean_b, ALU.subtract, [yb.r(), str_(B0 + 8, 4)], [yb.r()])
            tt(y3, y3, rstd_b, ALU.mult, [yb.r(), str_(B0 + 20, 4)], [yb.r()])
            tt(yrb.bf(), yb.f32(), gateb[c % 2].f32(), ALU.mult, [yb.r(), gateb[c % 2].r()], [yrb.r()])

        def S7(c):
            t_, c4 = divmod(c, 4)
            transpose_to(yrb, 4, yT4(t_)[:, c4, 4:8, :], [yT[t_ % 2].r(c4 * 2048 + 1024, (c4 + 1) * 2048)])

        def Vconv(t_, j):
            u4 = uT4(t_)
            ur = uT[t_ % 2].r()

            def mm_feat(wbuf, bank):
                wv = wbuf.bf("p (k n) -> p k n", k=8)
                for kc in range(8):
                    mm(psf(bank).rearrange("p (c n) -> p c n", c=4), wv[:, kc, j * 128:(j + 1) * 128], u4[:, :, kc, :],
                       kc == 0, kc == 7, [ur, wbuf.r()], [psr(bank)])

            bcc, bcx, bcb = nb(), nb(), nb()
            z = zb.f32()
            cv = cvb.f32()
            mm_feat(W(g, "cc"), bcc)
            act(ccb.f32(), psf(bcc), AF.Copy, [psr(bcc)], [ccb.r()])
            mm_feat(W(g, "cx"), bcx)
            hal = halob.f32()[:, j * 2:(j + 1) * 2]
            halr = halob.r(j * 8, (j + 1) * 8)
            act(z[:, 0:2], hal, AF.Copy, [halr], [zb.r(0, 8)])
            tt(z[:, 2:514], psf(bcx), ccb.f32(), ALU.mult, [psr(bcx), ccb.r()], [zb.r(8, 514 * 4)])
            act(hal, z[:, 512:514], AF.Copy, [zb.r(512 * 4, 514 * 4)], [halr])
            cw = convwb.f32()
            ts(cv, z[:, 2:514], cw[:, j * 3 + 2:j * 3 + 3], None, ALU.mult, None, [zb.r(), convwb.r()], [cvb.r()])
            stt(cv, z[:, 1:513], cw[:, j * 3 + 1:j * 3 + 2], cv, ALU.mult, ALU.add, [zb.r(), convwb.r(), cvb.r()], [cvb.r()])
            stt(cv, z[:, 0:512], cw[:, j * 3:j * 3 + 1], cv, ALU.mult, ALU.add, [zb.r(), convwb.r(), cvb.r()], [cvb.r()])
            mm_feat(W(g, "cb"), bcb)
            y4 = yT4(t_)[:, :, j, :]
            yregs = [yT[t_ % 2].r(c4 * 2048 + j * 256, c4 * 2048 + (j + 1) * 256) for c4 in range(4)]
            tt(y4, psf(bcb).rearrange("p (c n) -> p c n", c=4), cvb.f32("p (c n) -> p c n", c=4), ALU.mult, [psr(bcb), cvb.r()], yregs)

        def h_add(c, hf, src_ap, src_reg):
            hs = hbuf[c].f32()[:, hf * 512:(hf + 1) * 512]
            hr = hbuf[c].r(hf * 2048, (hf + 1) * 2048)
            tt(hs, hs, src_ap, ALU.add, [hr, src_reg], [hr])

        def S8(c):
            t_, c4 = divmod(c, 4)
            y4 = yT4(t_)
            yr_ = yT[t_ % 2].r(c4 * 2048, (c4 + 1) * 2048)
            for hf in range(2):
                b = nb()
                wb_ = W(g, "wo%d" % hf)
                wv = wb_.bf("p (k n) -> p k n", k=8)
                for m in range(8):
                    mm(psf(b), y4[:, c4, m, :], wv[:, m, :], m == 0, m == 7, [yr_, wb_.r()], [psr(b)])
                h_add(c, hf, psf(b), psr(b))

        S1(0)
        for c in range(NCH):
            if c + 1 < NCH:
                S1(c + 1)
            S2(c)
            S3(c)
            if c in (4, 5):
                Vconv(0, (c - 4) * 2)
            S4(c)
            if c in (4, 5):
                Vconv(0, (c - 4) * 2 + 1)
            if c in (6, 7):
                S8((c - 6) * 2)
            S5(c)
            if c >= 1:
                S7(c - 1)
            if c in (6, 7):
                S8((c - 6) * 2 + 1)
            S6(c)
        S7(NCH - 1)
        for nm in ("q", "k", "v", "g"):
            wfree(g, nm)
        for j in range(4):
            Vconv(1, j)
        for nm in ("cc", "cx", "cb"):
            wfree(g, nm)
        for c in range(4, 8):
            S8(c)
        wfree(g, "wo0")
        wfree(g, "wo1")

        if stop_phase >= 2:
            wk = Bump(arena, WORK_LO, WORK_LO + WORK_SIZE)
            gffnb = wk.alloc(4096)
            u2 = [wk.alloc(2048) for _ in range(2)]
            uT2 = wk.alloc(16384)
            sgf = [wk.alloc(2048) for _ in range(2)]
            aT = [wk.alloc(FFB * 1024) for _ in range(2)]
            load_g(gffnb, g_ffn_d, "g1")
            u2v = uT2.bf("p (c k n) -> p c k n", c=8, k=8)

            for c in range(NCH):
                norm_to_bf16(c, gffnb, u2[c % 2], 2 + c % 2)
                transpose_to(u2[c % 2], 8, u2v[:, c], [uT2.r(c * 2048, (c + 1) * 2048)])

            nblk = (NFF + FFB - 1) // FFB
            units = [(bi, t_) for bi in range(nblk) for t_ in range(2)]

            def ffn_up(ui):
                bi, t_ = units[ui]
                nf = min(FFB, NFF - bi * FFB)
                wg_, wu_ = W(g, "fg%d" % bi), W(g, "fu%d" % bi)
                wgv = wg_.bf("p (k n) -> p k n", k=8)
                wuv = wu_.bf("p (k n) -> p k n", k=8)
                a = aT[ui % 2]
                ur = uT2.r(t_ * 8192, (t_ + 1) * 8192)
                for f in range(nf):
                    bg_, bu_ = nb(), nb()
                    for (wv, wbuf, bank) in ((wgv, wg_, bg_), (wuv, wu_, bu_)):
                        for kc in range(8):
                            mm(psf(bank).rearrange("p (c n) -> p c n", c=4), wv[:, kc, f * 128:(f + 1) * 128],
                               u2v[:, t_ * 4:(t_ + 1) * 4, kc, :], kc == 0, kc == 7, [ur, wbuf.r()], [psr(bank)])
                    s_ = sgf[f % 2]
                    act(s_.f32(), psf(bg_), AF.Silu, [psr(bg_)], [s_.r()])
                    tt(a.bf()[:, f * 512:(f + 1) * 512], psf(bu_), s_.f32(), ALU.mult, [psr(bu_), s_.r()], [a.r(f * 1024, (f + 1) * 1024)])

            def ffn_down(ui):
                bi, t_ = units[ui]
                nf = min(FFB, NFF - bi * FFB)
                wd_ = W(g, "fd%d" % bi)
                wdv = wd_.bf("p (f n) -> p f n", f=nf)
                a = aT[ui % 2]
                av = a.bf("p (f n) -> p f n", f=FFB)
                for c4 in range(4):
                    c = t_ * 4 + c4
                    for hf in range(2):
                        b = nb()
                        for f in range(nf):
                            mm(psf(b), av[:, f, c4 * 128:(c4 + 1) * 128], wdv[:, f, hf * 512:(hf + 1) * 512], f == 0, f == nf - 1,
                               [a.r(), wd_.r()], [psr(b)])
                        h_add(c, hf, psf(b), psr(b))
                if t_ == 1:
                    wfree(g, "fg%d" % bi)
                    wfree(g, "fu%d" % bi)
                    wfree(g, "fd%d" % bi)

            for ui in range(len(units)):
                ffn_up(ui)
                if ui >= 1:
                    ffn_down(ui - 1)
            ffn_down(len(units) - 1)

        wk = Bump(arena, WORK_LO, WORK_LO + WORK_SIZE)
        gpleb = wk.alloc(4096)
        gfinb = wk.alloc(4096)
        u3 = [wk.alloc(2048) for _ in range(2)]
        uT3 = [wk.alloc(2048) for _ in range(2)]
        pbf = [wk.alloc(512) for _ in range(2)]
        pT = [wk.alloc(512) for _ in range(2)]
        sig = [wk.alloc(2048) for _ in range(2)]
        tb = [wk.alloc(2048) for _ in range(2)]
        junkb = wk.alloc(2048)
        if stop_phase >= 3:
            load_g(gpleb, g_ple_d, "g2")
            load_g(gfinb, g_fin_d, "g3")

        def N3a(c):
            dma("pool", pbf[c % 2].bf(), p_d[tok0 + c * 128: tok0 + (c + 1) * 128, :], [], [pbf[c % 2].r()], "p%d" % (c % 2))
            norm_to_bf16(c, gpleb, u3[c % 2], 4 + c % 2)

        def N3b(c):
            transpose_to(u3[c % 2], 8, uT3[c % 2].bf("p (k n) -> p k n", k=8), [uT3[c % 2].r()])
            transpose_to(pbf[c % 2], 2, pT[c % 2].bf("p (k n) -> p k n", k=2), [pT[c % 2].r()])

        def G3(c):
            ut = uT3[c % 2].bf("p (k n) -> p k n", k=8)
            pt = pT[c % 2].bf("p (k n) -> p k n", k=2)
            wpp_ = W(g, "pp")
            wppv = wpp_.bf("p (k n) -> p k n", k=2)
            for hf in range(2):
                bg_, bp_ = nb(), nb()
                wg_ = W(g, "pg%d" % hf)
                wgv = wg_.bf("p (k n) -> p k n", k=8)
                for kc in range(8):
                    mm(psf(bg_), ut[:, kc, :], wgv[:, kc, :], kc == 0, kc == 7, [uT3[c % 2].r(), wg_.r()], [psr(bg_)])
                for kc in range(2):
                    mm(psf(bp_), pt[:, kc, :], wppv[:, kc, hf * 512:(hf + 1) * 512], kc == 0, kc == 1, [pT[c % 2].r(), wpp_.r()], [psr(bp_)])
                s_ = sig[hf]
                t2 = tb[hf]
                act(s_.f32(), psf(bg_), AF.Sigmoid, [psr(bg_)], [s_.r()])
                tt(t2.f32(), psf(bp_), s_.f32(), ALU.mult, [psr(bp_), s_.r()], [t2.r()])
                h_add(c, hf, t2.f32(), t2.r())

        def F3(c, do_norm=True):
            h = hbuf[c].f32()
            if do_norm:
                ss_i, rs_i = 8 + (c % 2) * 2, 9 + (c % 2) * 2
                act(junkb.bf(), h, AF.Square, [hbuf[c].r()], [junkb.r(), str_(ss_i)], accum_out=st(ss_i))
                rstd_from_ss(ss_i, rs_i, D)
                stt(h, h, st(rs_i), gfinb.f32(), ALU.mult, ALU.mult, [hbuf[c].r(), str_(rs_i), gfinb.r()], [hbuf[c].r()])
            dma("sp", out_d[tok0 + c * 128: tok0 + (c + 1) * 128, :], h, [hbuf[c].r()], [], "h%d" % c)

        if stop_phase >= 3:
            N3a(0)
            for c in range(NCH):
                if c + 1 < NCH:
                    N3a(c + 1)
                N3b(c)
                if c >= 1:
                    G3(c - 1)
                    F3(c - 1)
            G3(NCH - 1)
            F3(NCH - 1)
            wfree(g, "pg0")
            wfree(g, "pg1")
            wfree(g, "pp")
        else:
            for c in range(NCH):
                F3(c, do_norm=False)

    assert not pending and not loaded, (len(pending), list(loaded))

    sem_names = ["sp", "act", "pool", "pe", "dve"]
    sems = {n: es.enter_context(nc.semaphore("s_" + n)) for n in sem_names}
    dma_sems = {k: es.enter_context(nc.semaphore("d_" + k)) for k in sorted(dma_keys)}
    block = es.enter_context(nc.Block())
    P.emit(nc, block, sems, dma_sems)
    es.close()
    nc._prog_stats = dict(nops=len(P.ops), ticks=P.final_cnt)
    return nc


def _tables():
    half = 64
    inv_freq = (10000.0 ** (-np.arange(half, dtype=np.float32) / half)).astype(np.float32)
    pos = np.arange(SEQ, dtype=np.float32)
    ang = (pos[:, None] * inv_freq[None, :]).astype(np.float32)
    cos = np.cos(ang).astype(np.float32)
    sin = np.sin(ang).astype(np.float32)
    cos_t = cos.reshape(16, 128, 64).transpose(1, 0, 2).reshape(128, 16 * 64)
    sin_t = np.stack([-sin, sin], axis=1).reshape(16, 128, 2, 64).transpose(1, 0, 2, 3).reshape(128, 16 * 128)
    lg = np.log(1.0 - np.power(2.0, -5.0 - np.arange(HEADS, dtype=np.float64)))
    idx = np.arange(128, dtype=np.float64)
    scale = 128.0 ** -0.5
    diff = idx[None, :] - idx[:, None]
    maskT = np.zeros((128, HEADS, 128), np.float64)
    for h in range(HEADS):
        maskT[:, h, :] = np.where(diff >= 0, np.exp(lg[h] * np.maximum(diff, 0.0)), 0.0) * scale
    kdec = np.zeros((128, HEADS, 128), np.float64)
    qdec = np.zeros((128, HEADS, 128), np.float64)
    for h in range(HEADS):
        kdec[:, h, :] = (np.exp(lg[h] * (127.0 - idx)) * scale)[:, None]
        qdec[:, h, :] = np.exp(lg[h] * (idx + 1.0))[:, None]
    return (np.ascontiguousarray(cos_t, np.float32), np.ascontiguousarray(sin_t, np.float32),
            maskT.reshape(128, 512).astype(np.float32), kdec.reshape(128, 512).astype(np.float32),
            qdec.reshape(128, 512).astype(np.float32))


_NC_CACHE = {}


def kernel(x, p, g_mix, w_in, conv_w, ret_gn, w_out, g_ffn, w_gate, w_up, w_down, g_ple,
           w_ple_gate, w_ple_proj, g_final):
    f = lambda a: np.ascontiguousarray(np.asarray(a, dtype=np.float32))
    x = f(x).reshape(NCORE, TOKC, D)
    p = f(p)[0].reshape(NCORE, TOKC, PLE)
    cos_t, sin_t, maskT, kdec, qdec = _tables()
    convw = np.ascontiguousarray(f(conv_w)[0].reshape(3, 4, 128).transpose(2, 1, 0).reshape(128, 12))
    ident = np.eye(128, dtype=np.float32).astype(ml_dtypes.bfloat16)
    shared = {
        "w_in": f(w_in)[0], "w_out": f(w_out)[0], "w_gate": f(w_gate)[0], "w_up": f(w_up)[0],
        "w_down": f(w_down)[0], "w_ple_gate": f(w_ple_gate)[0], "w_ple_proj": f(w_ple_proj)[0],
        "g_mix": f(g_mix)[0], "g_ffn": f(g_ffn)[0], "g_ple": f(g_ple)[0], "g_final": f(g_final),
        "ret_gn": f(ret_gn)[0], "convw": convw, "ident": ident, "cos_t": cos_t, "sin_t": sin_t,
        "maskT": maskT, "kdec": kdec, "qdec": qdec,
    }
    if "nc" not in _NC_CACHE:
        _NC_CACHE["nc"] = build_nc()
    nc = _NC_CACHE["nc"]
    in_maps = [dict(shared, x=x[i], p=p[i]) for i in range(NCORE)]
    res = run_bass_kernel_spmd(nc, in_maps, core_ids=list(range(NCORE)))
    out = np.stack([np.asarray(r["out"], dtype=np.float32) for r in res.results], axis=0)
    return out.reshape(16, SEQ, D)
```

```python
import numpy as np
import ml_dtypes
import concourse.bass as bass
import concourse.mybir as mybir
from concourse.bass_utils import run_bass_kernel_spmd

F32 = mybir.dt.float32
BF16 = mybir.dt.bfloat16
AF = mybir.ActivationFunctionType
ALU = mybir.AluOpType
AX = mybir.AxisListType

D = 1024
SEQ = 2048
NCORE = 8
TOKC = 4096
GT = 1024
NCH = 8
NG = TOKC // GT
INC = 3584
DFF = 2816
NFF = 22
PLE = 256
EPS = 1e-6
HEADS = 4
FFB = 4
ARENA_BYTES = 206 * 1024


class Op:
    __slots__ = ("eng", "fn", "deps", "is_dma", "key", "signal", "tick", "idx")


class Prog:
    def __init__(self):
        self.ops = []
        self.recs = {}
        self.dma_count = {}

    def _access(self, reg, idx, is_write, out):
        space, lo, hi = reg
        lst = self.recs.setdefault(space, [])
        me = self.ops_eng(idx)
        keep = []
        excl = space == "ps"
        for rec in lst:
            rlo, rhi, ridx, rw = rec
            if rhi <= lo or rlo >= hi:
                keep.append(rec)
                continue
            if ridx != idx and (is_write or rw or (excl and self.ops_eng(ridx) != me)):
                kind = "RAW" if (rw and not is_write) else "X"
                out.append((ridx, kind))
            if is_write and rlo >= lo and rhi <= hi:
                continue
            if (not is_write) and (not rw) and rlo == lo and rhi == hi and self.ops_eng(ridx) == me:
                continue
            keep.append(rec)
        if (not is_write) and space == "sb" and not any(rec[3] and rec[0] < hi and rec[1] > lo for rec in lst):
            print("WARNING: read of never-written region", reg, "op", idx, self._cur_eng)
        keep.append([lo, hi, idx, is_write])
        self.recs[space] = keep

    def ops_eng(self, idx):
        if idx == len(self.ops):
            return self._cur_eng
        o = self.ops[idx]
        return ("dma", o.idx) if o.is_dma else o.eng

    def add(self, eng, fn, r=(), w=(), dma_key=None):
        op = Op()
        op.eng = eng
        op.fn = fn
        op.is_dma = dma_key is not None
        op.key = dma_key
        op.signal = op.is_dma
        op.tick = 0
        op.idx = len(self.ops)
        self._cur_eng = ("dma", op.idx) if op.is_dma else eng
        raw = []
        for reg in r:
            self._access(reg, op.idx, False, raw)
        for reg in w:
            self._access(reg, op.idx, True, raw)
        deps = {}
        for (pidx, kind) in raw:
            p = self.ops[pidx]
            if p.is_dma:
                deps[pidx] = True
                continue
            if (not op.is_dma) and p.eng == eng:
                if eng == "pe":
                    continue
            deps[pidx] = True
        best = {}
        final = []
        for pidx in deps:
            p = self.ops[pidx]
            if p.is_dma:
                final.append(pidx)
            else:
                if p.eng not in best or best[p.eng] < pidx:
                    best[p.eng] = pidx
        final.extend(best.values())
        for pidx in final:
            self.ops[pidx].signal = True
        op.deps = final
        self.ops.append(op)
        return op

    def emit(self, nc, block, sems, dma_sems):
        cnt = {}
        for op in self.ops:
            if op.is_dma:
                c = self.dma_count.get(op.key, 0) + 16
                self.dma_count[op.key] = c
                op.tick = c
            elif op.signal:
                c = cnt.get(op.eng, 0) + 1
                cnt[op.eng] = c
                op.tick = c
        self.final_cnt = cnt
        ops = self.ops

        def run(engname, e):
            waited = {}
            for op in ops:
                if op.eng != engname:
                    continue
                for pidx in op.deps:
                    p = ops[pidx]
                    if p.is_dma:
                        sem = dma_sems[p.key]
                        sk = ("d", p.key)
                    else:
                        sem = sems[p.eng]
                        sk = ("e", p.eng)
                    if waited.get(sk, 0) >= p.tick:
                        continue
                    waited[sk] = p.tick
                    e.wait_ge(sem, p.tick)
                ins = op.fn(e)
                if op.is_dma:
                    ins.then_inc(dma_sems[op.key], 16)
                elif op.signal:
                    ins.then_inc(sems[op.eng], 1)
            if engname == "sp":
                for k, c in self.dma_count.items():
                    if waited.get(("d", k), 0) < c:
                        e.wait_ge(dma_sems[k], c)

        @block.sync
        def _(e):
            run("sp", e)

        @block.scalar
        def _(e):
            run("act", e)

        @block.gpsimd
        def _(e):
            run("pool", e)

        @block.tensor
        def _(e):
            run("pe", e)

        @block.vector
        def _(e):
            run("dve", e)


class Buf:
    def __init__(self, arena, lo, n):
        self.arena = arena
        self.lo = lo
        self.n = n

    def f32(self, pat=None, **kw):
        a = self.arena[:, self.lo // 4:(self.lo + self.n) // 4]
        return a.rearrange(pat, **kw) if pat else a

    def bf(self, pat=None, **kw):
        a = self.arena[:, self.lo // 4:(self.lo + self.n) // 4].bitcast(BF16)
        return a.rearrange(pat, **kw) if pat else a

    def r(self, lo=0, hi=None):
        return ("sb", self.lo + lo, self.lo + (self.n if hi is None else hi))


class Bump:
    def __init__(self, arena, lo, hi):
        self.arena, self.lo, self.hi, self.p = arena, lo, hi, lo

    def alloc(self, n):
        n = (n + 63) // 64 * 64
        b = Buf(self.arena, self.p, n)
        self.p += n
        assert self.p <= self.hi, ("arena overflow", self.p, self.hi)
        return b


class Ring:
    def __init__(self, arena, lo, size):
        self.arena, self.lo, self.size = arena, lo, size
        self.head = 0
        self.live = []

    def try_alloc(self, n):
        start = self.head if self.head + n <= self.size else 0
        if all(start + n <= o or start >= o + m for (o, m) in self.live):
            self.head = start + n
            self.live.append((start, n))
            return Buf(self.arena, self.lo + start, n)
        return None

    def free(self, buf):
        self.live.remove((buf.lo - self.lo, buf.n))


def build_nc(ngroups=NG, stop_phase=3):
    nc = bass.Bass("TRN2", target_bir_lowering=False)
    dt_in = lambda name, shape, dt=F32: nc.dram_tensor(name, shape, dt, kind="ExternalInput").ap()
    x_d = dt_in("x", [TOKC, D])
    p_d = dt_in("p", [TOKC, PLE])
    w_in_d = dt_in("w_in", [D, INC])
    w_out_d = dt_in("w_out", [D, D])
    w_gate_d = dt_in("w_gate", [D, DFF])
    w_up_d = dt_in("w_up", [D, DFF])
    w_down_d = dt_in("w_down", [DFF, D])
    w_pg_d = dt_in("w_ple_gate", [D, D])
    w_pp_d = dt_in("w_ple_proj", [PLE, D])
    g_mix_d = dt_in("g_mix", [D])
    g_ffn_d = dt_in("g_ffn", [D])
    g_ple_d = dt_in("g_ple", [D])
    g_fin_d = dt_in("g_final", [D])
    retgn_d = dt_in("ret_gn", [512])
    convw_d = dt_in("convw", [128, 12])
    ident_d = dt_in("ident", [128, 128], BF16)
    cos_d = dt_in("cos_t", [128, 16 * 64])
    sin_d = dt_in("sin_t", [128, 16 * 128])
    mask_d = dt_in("maskT", [128, 512])
    kdec_d = dt_in("kdec", [128, 512])
    qdec_d = dt_in("qdec", [128, 512])
    out_d = nc.dram_tensor("out", [TOKC, D], F32, kind="ExternalOutput").ap()

    log_gamma = np.log(1.0 - np.power(2.0, -5.0 - np.arange(HEADS, dtype=np.float64)))
    cdec = [float(np.exp(lg * 128.0)) for lg in log_gamma]

    P = Prog()
    from contextlib import ExitStack
    es = ExitStack()
    arena_t = es.enter_context(nc.sbuf_tensor("arena", [128, ARENA_BYTES // 4], F32))
    ps_t = es.enter_context(nc.psum_tensor("ps", [128, 8, 512], F32))
    arena = arena_t[:]
    pers = Bump(arena, 0, ARENA_BYTES)

    hbuf = [pers.alloc(4096) for _ in range(NCH)]
    cosb = pers.alloc(NCH * 64 * 4)
    sinb = pers.alloc(NCH * 128 * 4)
    maskb = pers.alloc(2048)
    kdecb = pers.alloc(2048)
    qdecb = pers.alloc(2048)
    retgnb = pers.alloc(2048)
    identb = pers.alloc(256)
    stateb = pers.alloc(2048)
    statebf = pers.alloc(1024)
    convwb = pers.alloc(64)
    halob = pers.alloc(64)
    statb = pers.alloc(256)
    epsb = pers.alloc(64)
    WORK_LO = pers.p
    WORK_SIZE = 75 * 1024
    RING_LO = WORK_LO + WORK_SIZE
    RING_SIZE = ARENA_BYTES - RING_LO
    ring = Ring(arena, RING_LO, RING_SIZE)
    ident = identb.bf()

    ps_ctr = [0]

    def nb():
        b = ps_ctr[0] % 8
        ps_ctr[0] += 1
        return b

    def psr(b):
        return ("ps", b, b + 1)

    def psf(b):
        return ps_t[:, b, :]

    def psb(b):
        return ps_t[:, b, :].bitcast(BF16)

    dma_keys = set()

    def dma(eng, out, in_, r, w, key):
        dma_keys.add(key)
        P.add(eng, lambda e: e.dma_start(out=out, in_=in_), r=r, w=w, dma_key=key)

    def mm(out, lhsT, rhs, start, stop, r, w):
        P.add("pe", lambda e: e.matmul(out, lhsT=lhsT, rhs=rhs, start=start, stop=stop), r=r, w=w)

    def tr(out, in_, r, w):
        P.add("pe", lambda e: e.transpose(out=out, in_=in_, identity=ident), r=list(r) + [identb.r()], w=w)

    def tt(out, in0, in1, op, r, w):
        P.add("dve", lambda e: e.tensor_tensor(out=out, in0=in0, in1=in1, op=op), r=r, w=w)

    def stt(out, in0, scalar, in1, op0, op1, r, w):
        P.add("dve", lambda e: e.scalar_tensor_tensor(out=out, in0=in0, scalar=scalar, in1=in1, op0=op0, op1=op1), r=r, w=w)

    def ts(out, in0, s1, s2, op0, op1, r, w):
        if op1 is None:
            P.add("dve", lambda e: e.tensor_scalar(out=out, in0=in0, scalar1=s1, scalar2=None, op0=op0), r=r, w=w)
        else:
            P.add("dve", lambda e: e.tensor_scalar(out=out, in0=in0, scalar1=s1, scalar2=s2, op0=op0, op1=op1), r=r, w=w)

    def rsum(out, in_, r, w):
        P.add("dve", lambda e: e.reduce_sum(out=out, in_=in_, axis=AX.X), r=r, w=w)

    def mset(ap, val, w):
        P.add("dve", lambda e: e.memset(ap, val), r=[], w=w)

    def act(out, in_, func, r, w, accum_out=None):
        if accum_out is None:
            P.add("act", lambda e: e.activation(out=out, in_=in_, func=func), r=r, w=w)
        else:
            P.add("act", lambda e: e.activation(out=out, in_=in_, func=func, accum_out=accum_out), r=r, w=w)

    cidx = [0]

    def load_const(buf, src, bf=False):
        dst = buf.bf() if bf else buf.f32()
        dst = dst[:, 0:src.shape[1]]
        dma("sp", dst, src, [], [buf.r()], "c%d" % cidx[0])
        cidx[0] += 1

    P.add("dve", lambda e: e.memset(epsb.f32(), EPS), r=[], w=[epsb.r()])
    load_const(identb, ident_d, bf=True)
    load_const(maskb, mask_d)
    load_const(kdecb, kdec_d)
    load_const(qdecb, qdec_d)
    load_const(convwb, convw_d)
    load_const(retgnb, retgn_d.partition_broadcast(128))

    def wview(d, c0, c1):
        return d.rearrange("(kc p) n -> p kc n", p=128)[:, :, c0:c1]

    def piece_list(g):
        L = []
        for nm, c0 in (("q", 1536), ("k", 2048), ("v", 2560), ("g", 3072), ("cc", 512), ("cx", 1024), ("cb", 0)):
            L.append(((g, nm), wview(w_in_d, c0, c0 + 512), 8 * 512 * 2))
        for hf in range(2):
            L.append(((g, "wo%d" % hf), wview(w_out_d, hf * 512, hf * 512 + 512), 8 * 512 * 2))
        if stop_phase >= 2:
            f0 = 0
            bi = 0
            while f0 < NFF:
                nf = min(FFB, NFF - f0)
                L.append(((g, "fg%d" % bi), wview(w_gate_d, f0 * 128, (f0 + nf) * 128), 8 * nf * 128 * 2))
                L.append(((g, "fu%d" % bi), wview(w_up_d, f0 * 128, (f0 + nf) * 128), 8 * nf * 128 * 2))
                L.append(((g, "fd%d" % bi), w_down_d.rearrange("(f p) n -> p f n", p=128)[:, f0:f0 + nf, :], nf * 1024 * 2))
                f0 += nf
                bi += 1
        if stop_phase >= 3:
            for hf in range(2):
                L.append(((g, "pg%d" % hf), wview(w_pg_d, hf * 512, hf * 512 + 512), 8 * 512 * 2))
            L.append(((g, "pp"), w_pp_d.rearrange("(k p) n -> p k n", p=128), 2 * 1024 * 2))
        return L

    pending = []
    for g in range(ngroups):
        pending.extend(piece_list(g))
    pending.reverse()
    loaded = {}
    wsem_ctr = [0]
    NWSEM = 24

    def pump():
        while pending:
            name, src, nbytes = pending[-1]
            b = ring.try_alloc(nbytes)
            if b is None:
                return
            pending.pop()
            loaded[name] = b
            dst = b.bf("p (a n) -> p a n", a=src.shape[1])
            key = "w%d" % (wsem_ctr[0] % NWSEM)
            wsem_ctr[0] += 1
            dma("pool", dst, src, [], [b.r()], key)

    def W(g, nm):
        pump()
        assert (g, nm) in loaded, ("weight piece not resident (ring too small?)", g, nm, ring.live)
        return loaded[(g, nm)]

    def wfree(g, nm):
        ring.free(loaded.pop((g, nm)))
        pump()

    stat = statb.f32()

    def st(i, n=1):
        return stat[:, i:i + n]

    def str_(i, n=1):
        return statb.r(i * 4, (i + n) * 4)

    def rstd_from_ss(ss_i, rs_i, n, w=1):
        sq = st(rs_i, w)
        P.add("act", lambda e: e.activation(out=sq, in_=st(ss_i, w), func=AF.Sqrt, bias=epsb.f32()[:, 0:1], scale=1.0 / n),
              r=[str_(ss_i, w), epsb.r()], w=[str_(rs_i, w)])
        P.add("dve", lambda e: e.reciprocal(out=sq, in_=sq), r=[str_(rs_i, w)], w=[str_(rs_i, w)])

    def norm_to_bf16(c, gbuf, ubuf, slot):
        h = hbuf[c].f32()
        ss_i, rs_i = slot * 2, slot * 2 + 1
        act(ubuf.bf(), h, AF.Square, [hbuf[c].r()], [ubuf.r(), str_(ss_i)], accum_out=st(ss_i))
        rstd_from_ss(ss_i, rs_i, D)
        stt(ubuf.bf(), h, st(rs_i), gbuf.f32(), ALU.mult, ALU.mult, [hbuf[c].r(), str_(rs_i), gbuf.r()], [ubuf.r()])

    def transpose_to(ubuf, nk, dst_ap, dst_regs, eng="act"):
        b = nb()
        u = ubuf.bf()
        for k in range(nk):
            tr(psb(b)[:, k * 128:(k + 1) * 128], u[:, k * 128:(k + 1) * 128], [ubuf.r(k * 256, (k + 1) * 256)], [psr(b)])
        src = psb(b)[:, 0:nk * 128].rearrange("p (k n) -> p k n", k=nk)
        if eng == "act":
            act(dst_ap, src, AF.Copy, [psr(b)], dst_regs)
        else:
            P.add("dve", lambda e: e.tensor_copy(out=dst_ap, in_=src), r=[psr(b)], w=dst_regs)

    def load_g(buf, src, key):
        dma("sp", buf.f32(), src.partition_broadcast(128), [], [buf.r()], key)

    for g in range(ngroups):
        seq_first = (g % 2 == 0)
        tok0 = g * GT
        pos_chunk0 = (g % 2) * NCH

        dma("sp", cosb.f32(), cos_d[:, pos_chunk0 * 64:(pos_chunk0 + NCH) * 64], [], [cosb.r()], "tabc")
        dma("sp", sinb.f32(), sin_d[:, pos_chunk0 * 128:(pos_chunk0 + NCH) * 128], [], [sinb.r()], "tabs")

        if seq_first:
            mset(stateb.f32(), 0.0, [stateb.r()])
            mset(statebf.bf(), 0.0, [statebf.r()])
            mset(halob.f32(), 0.0, [halob.r()])

        wk = Bump(arena, WORK_LO, WORK_LO + WORK_SIZE)
        gmixb = wk.alloc(4096)
        ub = [wk.alloc(2048) for _ in range(2)]
        uT = [wk.alloc(8192) for _ in range(2)]
        yT = [wk.alloc(8192) for _ in range(2)]
        ccb = wk.alloc(2048)
        zb = wk.alloc(514 * 4)
        cvb = wk.alloc(2048)
        TAIL_LIVE = wk.p - WORK_LO
        ropeA = wk.alloc(2048)
        ropeB = wk.alloc(2048)
        qksb = wk.alloc(2048)
        qkT = [wk.alloc(2048) for _ in range(2)]
        vsb = [wk.alloc(1024) for _ in range(2)]
        gateb = [wk.alloc(2048) for _ in range(2)]
        ksb = wk.alloc(1024)
        scsb = [wk.alloc(1024) for _ in range(2)]
        y1b = wk.alloc(2048)
        yb = wk.alloc(2048)
        yrb = wk.alloc(1024)
        wk2 = Bump(arena, WORK_LO + TAIL_LIVE, WORK_LO + WORK_SIZE)
        gffnb = wk2.alloc(4096)
        u2 = [wk2.alloc(2048) for _ in range(4)]
        uT2 = wk2.alloc(16384)
        wk2b = Bump(arena, WORK_LO, WORK_LO + TAIL_LIVE)
        sgf = [wk2b.alloc(2048) for _ in range(2)]
        aT = [wk2b.alloc(FFB * 1024) for _ in range(2)]
        u2v = uT2.bf("p (c k n) -> p c k n", c=8, k=8)
        load_g(gmixb, g_mix_d, "g0")

        def uT4(t_):
            return uT[t_ % 2].bf("p (c k n) -> p c k n", c=4, k=8)

        def yT4(t_):
            return yT[t_ % 2].bf("p (c m n) -> p c m n", c=4, m=8)

        def S1(c):
            dma("sp", hbuf[c].f32(), x_d[tok0 + c * 128: tok0 + (c + 1) * 128, :], [], [hbuf[c].r()], "h%d" % c)
            norm_to_bf16(c, gmixb, ub[c % 2], c % 2)

        def S2(c):
            t_, c4 = divmod(c, 4)
            transpose_to(ub[c % 2], 8, uT4(t_)[:, c4], [uT[t_ % 2].r(c4 * 2048, (c4 + 1) * 2048)])

        def mm_tok(c, wbuf, bank):
            t_, c4 = divmod(c, 4)
            wv = wbuf.bf("p (k n) -> p k n", k=8)
            ur = uT[t_ % 2].r(c4 * 2048, (c4 + 1) * 2048)
            for kc in range(8):
                mm(psf(bank), uT4(t_)[:, c4, kc, :], wv[:, kc, :], kc == 0, kc == 7, [ur, wbuf.r()], [psr(bank)])

        def S3(c):
            bq, bk, bv, bg = nb(), nb(), nb(), nb()
            mm_tok(c, W(g, "q"), bq)
            mm_tok(c, W(g, "k"), bk)
            mm_tok(c, W(g, "v"), bv)
            mm_tok(c, W(g, "g"), bg)
            cosv = cosb.f32("p (c d) -> p c d", c=NCH)[:, c, :].unsqueeze(1).unsqueeze(1).broadcast_to([128, 4, 2, 64])
            sinv = sinb.f32("p (c t d) -> p c t d", c=NCH, t=2)
            a4 = ropeA.f32("p (h t d) -> p h t d", h=4, t=2)
            b4 = ropeB.f32("p (h t d) -> p h t d", h=4, t=2)
            for (bank, off) in ((bq, 0), (bk, 512)):
                x4 = psf(bank).rearrange("p (h t d) -> p h t d", h=4, t=2)
                tt(a4, x4, cosv, ALU.mult, [psr(bank), cosb.r()], [ropeA.r()])
                for t2 in range(2):
                    sv = sinv[:, c, t2, :].unsqueeze(1).broadcast_to([128, 4, 64])
                    tt(b4[:, :, t2, :], x4[:, :, 1 - t2, :], sv, ALU.mult, [psr(bank), sinb.r()], [ropeB.r()])
                tt(qksb.bf()[:, off:off + 512], ropeA.f32(), ropeB.f32(), ALU.add, [ropeA.r(), ropeB.r()],
                   [qksb.r(off * 2, (off + 512) * 2)])
            tt(ksb.bf(), qksb.bf()[:, 512:1024], kdecb.f32(), ALU.mult, [qksb.r(1024, 2048), kdecb.r()], [ksb.r()])
            act(vsb[c % 2].bf(), psf(bv), AF.Copy, [psr(bv)], [vsb[c % 2].r()])
            act(gateb[c % 2].f32(), psf(bg), AF.Silu, [psr(bg)], [gateb[c % 2].r()])
            tt(gateb[c % 2].f32(), gateb[c % 2].f32(), retgnb.f32(), ALU.mult, [gateb[c % 2].r(), retgnb.r()], [gateb[c % 2].r()])

        def S4(c):
            transpose_to(qksb, 8, qkT[c % 2].bf("p (k n) -> p k n", k=8), [qkT[c % 2].r()])

        def S5(c):
            b = nb()
            qk = qkT[c % 2].bf("p (k n) -> p k n", k=8)
            for h in range(HEADS):
                mm(psf(b)[:, h * 128:(h + 1) * 128], qk[:, 4 + h, :], qk[:, h, :], True, True, [qkT[c % 2].r()], [psr(b)])
            tt(scsb[c % 2].bf(), psf(b), maskb.f32(), ALU.mult, [psr(b), maskb.r()], [scsb[c % 2].r()])

        def S6(c):
            boi, boc, bkv = nb(), nb(), nb()
            qk = qkT[c % 2].bf("p (k n) -> p k n", k=8)
            sc = scsb[c % 2].bf("p (h n) -> p h n", h=4)
            v = vsb[c % 2].bf()
            for h in range(HEADS):
                sl = slice(h * 128, (h + 1) * 128)
                mm(psf(boi)[:, sl], sc[:, h, :], v[:, sl], True, True, [scsb[c % 2].r(), vsb[c % 2].r()], [psr(boi)])
            for h in range(HEADS):
                sl = slice(h * 128, (h + 1) * 128)
                mm(psf(boc)[:, sl], qk[:, h, :], statebf.bf()[:, sl], True, True, [qkT[c % 2].r(), statebf.r()], [psr(boc)])
            for h in range(HEADS):
                sl = slice(h * 128, (h + 1) * 128)
                mm(psf(bkv)[:, sl], ksb.bf()[:, sl], v[:, sl], True, True, [ksb.r(), vsb[c % 2].r()], [psr(bkv)])
            tt(y1b.f32(), psf(boc), qdecb.f32(), ALU.mult, [psr(boc), qdecb.r()], [y1b.r()])
            tt(yb.f32(), psf(boi), y1b.f32(), ALU.add, [psr(boi), y1b.r()], [yb.r()])
            for h in range(HEADS):
                sl = slice(h * 128, (h + 1) * 128)
                sr = stateb.r(h * 512, (h + 1) * 512)
                stt(stateb.f32()[:, sl], stateb.f32()[:, sl], cdec[h], psf(bkv)[:, sl], ALU.mult, ALU.add, [sr, psr(bkv)], [sr])
            act(statebf.bf(), stateb.f32(), AF.Copy, [stateb.r()], [statebf.r()])
            y3 = yb.f32("p (h d) -> p h d", h=4)
            B0 = 16
            act(y1b.f32(), yb.f32(), AF.Square, [yb.r()], [y1b.r()])
            rsum(st(B0, 4), y3, [yb.r()], [str_(B0, 4)])
            rsum(st(B0 + 4, 4), y1b.f32("p (h d) -> p h d", h=4), [y1b.r()], [str_(B0 + 4, 4)])
            ts(st(B0 + 8, 4), st(B0, 4), 1.0 / 128, None, ALU.mult, None, [str_(B0, 4)], [str_(B0 + 8, 4)])
            tt(st(B0 + 12, 4), st(B0 + 8, 4), st(B0 + 8, 4), ALU.mult, [str_(B0 + 8, 4)], [str_(B0 + 12, 4)])
            stt(st(B0 + 16, 4), st(B0 + 4, 4), 1.0 / 128, st(B0 + 12, 4), ALU.mult, ALU.subtract,
                [str_(B0 + 4, 4), str_(B0 + 12, 4)], [str_(B0 + 16, 4)])
            rstd_from_ss(B0 + 16, B0 + 20, 1.0, w=4)
            mean_b = st(B0 + 8, 4).unsqueeze(2).broadcast_to([128, 4, 128])
            rstd_b = st(B0 + 20, 4).unsqueeze(2).broadcast_to([128, 4, 128])
            tt(y3, y3, mean_b, ALU.subtract, [yb.r(), str_(B0 + 8, 4)], [yb.r()])
            tt(y3, y3, rstd_b, ALU.mult, [yb.r(), str_(B0 + 20, 4)], [yb.r()])
            tt(yrb.bf(), yb.f32(), gateb[c % 2].f32(), ALU.mult, [yb.r(), gateb[c % 2].r()], [yrb.r()])

        def S7(c):
            t_, c4 = divmod(c, 4)
            transpose_to(yrb, 4, yT4(t_)[:, c4, 4:8, :], [yT[t_ % 2].r(c4 * 2048 + 1024, (c4 + 1) * 2048)])

        def Vconv(t_, j):
            u4 = uT4(t_)
            ur = uT[t_ % 2].r()

            def mm_feat(wbuf, bank):
                wv = wbuf.bf("p (k n) -> p k n", k=8)
                for kc in range(8):
                    mm(psf(bank).rearrange("p (c n) -> p c n", c=4), wv[:, kc, j * 128:(j + 1) * 128], u4[:, :, kc, :],
                       kc == 0, kc == 7, [ur, wbuf.r()], [psr(bank)])

            bcc, bcx, bcb = nb(), nb(), nb()
            z = zb.f32()
            cv = cvb.f32()
            mm_feat(W(g, "cc"), bcc)
            act(ccb.f32(), psf(bcc), AF.Copy, [psr(bcc)], [ccb.r()])
            mm_feat(W(g, "cx"), bcx)
            hal = halob.f32()[:, j * 2:(j + 1) * 2]
            halr = halob.r(j * 8, (j + 1) * 8)
            act(z[:, 0:2], hal, AF.Copy, [halr], [zb.r(0, 8)])
            tt(z[:, 2:514], psf(bcx), ccb.f32(), ALU.mult, [psr(bcx), ccb.r()], [zb.r(8, 514 * 4)])
            act(hal, z[:, 512:514], AF.Copy, [zb.r(512 * 4, 514 * 4)], [halr])
            cw = convwb.f32()
            ts(cv, z[:, 2:514], cw[:, j * 3 + 2:j * 3 + 3], None, ALU.mult, None, [zb.r(), convwb.r()], [cvb.r()])
            stt(cv, z[:, 1:513], cw[:, j * 3 + 1:j * 3 + 2], cv, ALU.mult, ALU.add, [zb.r(), convwb.r(), cvb.r()], [cvb.r()])
            stt(cv, z[:, 0:512], cw[:, j * 3:j * 3 + 1], cv, ALU.mult, ALU.add, [zb.r(), convwb.r(), cvb.r()], [cvb.r()])
            mm_feat(W(g, "cb"), bcb)
            y4 = yT4(t_)[:, :, j, :]
            yregs = [yT[t_ % 2].r(c4 * 2048 + j * 256, c4 * 2048 + (j + 1) * 256) for c4 in range(4)]
            tt(y4, psf(bcb).rearrange("p (c n) -> p c n", c=4), cvb.f32("p (c n) -> p c n", c=4), ALU.mult, [psr(bcb), cvb.r()], yregs)

        def h_add(c, hf, src_ap, src_reg):
            hs = hbuf[c].f32()[:, hf * 512:(hf + 1) * 512]
            hr = hbuf[c].r(hf * 2048, (hf + 1) * 2048)
            tt(hs, hs, src_ap, ALU.add, [hr, src_reg], [hr])

        def S8(c):
            t_, c4 = divmod(c, 4)
            y4 = yT4(t_)
            yr_ = yT[t_ % 2].r(c4 * 2048, (c4 + 1) * 2048)
            for hf in range(2):
                b = nb()
                wb_ = W(g, "wo%d" % hf)
                wv = wb_.bf("p (k n) -> p k n", k=8)
                for m in range(8):
                    mm(psf(b), y4[:, c4, m, :], wv[:, m, :], m == 0, m == 7, [yr_, wb_.r()], [psr(b)])
                h_add(c, hf, psf(b), psr(b))

        def N2a(c):
            norm_to_bf16(c, gffnb, u2[c % 4], 2 + c % 2)

        def N2b(c):
            transpose_to(u2[c % 4], 8, u2v[:, c], [uT2.r(c * 2048, (c + 1) * 2048)])

        S1(0)
        S1(1)
        S2(0)
        for c in range(NCH):
            if c + 2 < NCH:
                S1(c + 2)
            S3(c)
            if c + 1 < NCH:
                S2(c + 1)
            if c in (4, 5):
                Vconv(0, (c - 4) * 2)
            if c in (6, 7):
                S8((c - 6) * 2)
            S4(c)
            if c >= 1:
                S7(c - 1)
            S5(c)
            if c in (4, 5):
                Vconv(0, (c - 4) * 2 + 1)
            if c in (6, 7):
                S8((c - 6) * 2 + 1)
            S6(c)
        S7(NCH - 1)
        for nm in ("q", "k", "v", "g"):
            wfree(g, nm)
        p2 = stop_phase >= 2
        if p2:
            load_g(gffnb, g_ffn_d, "g1")
            N2a(0)
            N2a(1)
        Vconv(1, 0)
        if p2:
            N2b(0)
            N2a(2)
        Vconv(1, 1)
        if p2:
            N2b(1)
            N2a(3)
        Vconv(1, 2)
        if p2:
            N2b(2)
        Vconv(1, 3)
        if p2:
            N2b(3)
        for nm in ("cc", "cx", "cb"):
            wfree(g, nm)
        for c in range(4, 8):
            S8(c)
            if p2:
                N2a(c)
        wfree(g, "wo0")
        wfree(g, "wo1")

        if stop_phase >= 2:
            nblk = (NFF + FFB - 1) // FFB
            units = [(bi, t_) for bi in range(nblk) for t_ in range(2)]

            def ffn_up(ui):
                bi, t_ = units[ui]
                nf = min(FFB, NFF - bi * FFB)
                wg_, wu_ = W(g, "fg%d" % bi), W(g, "fu%d" % bi)
                wgv = wg_.bf("p (k n) -> p k n", k=8)
                wuv = wu_.bf("p (k n) -> p k n", k=8)
                a = aT[ui % 2]
                ur = uT2.r(t_ * 8192, (t_ + 1) * 8192)
                for f in range(nf):
                    bg_, bu_ = nb(), nb()
                    for (wv, wbuf, bank) in ((wgv, wg_, bg_), (wuv, wu_, bu_)):
                        for kc in range(8):
                            mm(psf(bank).rearrange("p (c n) -> p c n", c=4), wv[:, kc, f * 128:(f + 1) * 128],
                               u2v[:, t_ * 4:(t_ + 1) * 4, kc, :], kc == 0, kc == 7, [ur, wbuf.r()], [psr(bank)])
                    s_ = sgf[f % 2]
                    act(s_.f32(), psf(bg_), AF.Silu, [psr(bg_)], [s_.r()])
                    tt(a.bf()[:, f * 512:(f + 1) * 512], psf(bu_), s_.f32(), ALU.mult, [psr(bu_), s_.r()], [a.r(f * 1024, (f + 1) * 1024)])

            def ffn_down(ui):
                bi, t_ = units[ui]
                nf = min(FFB, NFF - bi * FFB)
                wd_ = W(g, "fd%d" % bi)
                wdv = wd_.bf("p (f n) -> p f n", f=nf)
                a = aT[ui % 2]
                av = a.bf("p (f n) -> p f n", f=FFB)
                for c4 in range(4):
                    c = t_ * 4 + c4
                    for hf in range(2):
                        b = nb()
                        for f in range(nf):
                            mm(psf(b), av[:, f, c4 * 128:(c4 + 1) * 128], wdv[:, f, hf * 512:(hf + 1) * 512], f == 0, f == nf - 1,
                               [a.r(), wd_.r()], [psr(b)])
                        h_add(c, hf, psf(b), psr(b))
                if t_ == 1:
                    wfree(g, "fg%d" % bi)
                    wfree(g, "fu%d" % bi)
                    wfree(g, "fd%d" % bi)

            for ui in range(len(units)):
                ffn_up(ui)
                if ui == 0:
                    for c in range(4, 8):
                        N2b(c)
                if ui >= 1:
                    ffn_down(ui - 1)
            ffn_down(len(units) - 1)

        wk = Bump(arena, WORK_LO, WORK_LO + WORK_SIZE)
        gpleb = wk.alloc(4096)
        gfinb = wk.alloc(4096)
        u3 = [wk.alloc(2048) for _ in range(2)]
        uT3 = [wk.alloc(2048) for _ in range(2)]
        pbf = [wk.alloc(512) for _ in range(2)]
        pT = [wk.alloc(512) for _ in range(2)]
        sig = [wk.alloc(2048) for _ in range(2)]
        tb = [wk.alloc(2048) for _ in range(2)]
        junkb = wk.alloc(2048)
        SS3, RS3, SSF, RSF = 32, 40, 48, 56

        def N3a(c):
            dma("pool", pbf[c % 2].bf(), p_d[tok0 + c * 128: tok0 + (c + 1) * 128, :], [], [pbf[c % 2].r()], "p%d" % (c % 2))
            h = hbuf[c].f32()
            stt(u3[c % 2].bf(), h, st(RS3 + c), gpleb.f32(), ALU.mult, ALU.mult, [hbuf[c].r(), str_(RS3 + c), gpleb.r()], [u3[c % 2].r()])

        def N3b(c):
            transpose_to(u3[c % 2], 8, uT3[c % 2].bf("p (k n) -> p k n", k=8), [uT3[c % 2].r()], eng="dve")
            transpose_to(pbf[c % 2], 2, pT[c % 2].bf("p (k n) -> p k n", k=2), [pT[c % 2].r()], eng="dve")

        def G3(c):
            ut = uT3[c % 2].bf("p (k n) -> p k n", k=8)
            pt = pT[c % 2].bf("p (k n) -> p k n", k=2)
            wpp_ = W(g, "pp")
            wppv = wpp_.bf("p (k n) -> p k n", k=2)
            for hf in range(2):
                bg_, bp_ = nb(), nb()
                wg_ = W(g, "pg%d" % hf)
                wgv = wg_.bf("p (k n) -> p k n", k=8)
                for kc in range(8):
                    mm(psf(bg_), ut[:, kc, :], wgv[:, kc, :], kc == 0, kc == 7, [uT3[c % 2].r(), wg_.r()], [psr(bg_)])
                for kc in range(2):
                    mm(psf(bp_), pt[:, kc, :], wppv[:, kc, hf * 512:(hf + 1) * 512], kc == 0, kc == 1, [pT[c % 2].r(), wpp_.r()], [psr(bp_)])
                s_ = sig[hf]
                t2 = tb[hf]
                act(s_.f32(), psf(bg_), AF.Sigmoid, [psr(bg_)], [s_.r()])
                tt(t2.f32(), psf(bp_), s_.f32(), ALU.mult, [psr(bp_), s_.r()], [t2.r()])
                h_add(c, hf, t2.f32(), t2.r())

        def sq_accum(c, col):
            act(junkb.bf(), hbuf[c].f32(), AF.Square, [hbuf[c].r()], [junkb.r(), str_(col)], accum_out=st(col))

        def F3_tile(t_):
            cs = range(t_ * 4, t_ * 4 + 4)
            for c in cs:
                sq_accum(c, SSF + c)
            rstd_from_ss(SSF + t_ * 4, RSF + t_ * 4, D, w=4)
            for c in cs:
                h = hbuf[c].f32()
                stt(h, h, st(RSF + c), gfinb.f32(), ALU.mult, ALU.mult, [hbuf[c].r(), str_(RSF + c), gfinb.r()], [hbuf[c].r()])
                dma("sp", out_d[tok0 + c * 128: tok0 + (c + 1) * 128, :], h, [hbuf[c].r()], [], "h%d" % c)

        if stop_phase >= 3:
            load_g(gpleb, g_ple_d, "g2")
            load_g(gfinb, g_fin_d, "g3")
            for c in range(NCH):
                sq_accum(c, SS3 + c)
            rstd_from_ss(SS3, RS3, D, w=8)
            N3a(0)
            for c in range(NCH):
                if c + 1 < NCH:
                    N3a(c + 1)
                N3b(c)
                if c >= 1:
                    G3(c - 1)
                if c == 4:
                    F3_tile(0)
            G3(NCH - 1)
            F3_tile(1)
            wfree(g, "pg0")
            wfree(g, "pg1")
            wfree(g, "pp")
        else:
            for c in range(NCH):
                dma("sp", out_d[tok0 + c * 128: tok0 + (c + 1) * 128, :], hbuf[c].f32(), [hbuf[c].r()], [], "h%d" % c)

    assert not pending and not loaded, (len(pending), list(loaded))

    sem_names = ["sp", "act", "pool", "pe", "dve"]
    sems = {n: es.enter_context(nc.semaphore("s_" + n)) for n in sem_names}
    dma_sems = {k: es.enter_context(nc.semaphore("d_" + k)) for k in sorted(dma_keys)}
    block = es.enter_context(nc.Block())
    P.emit(nc, block, sems, dma_sems)
    es.close()
    nc._prog_stats = dict(nops=len(P.ops), ticks=P.final_cnt)
    return nc


def _tables():
    half = 64
    inv_freq = (10000.0 ** (-np.arange(half, dtype=np.float32) / half)).astype(np.float32)
    pos = np.arange(SEQ, dtype=np.float32)
    ang = (pos[:, None] * inv_freq[None, :]).astype(np.float32)
    cos = np.cos(ang).astype(np.float32)
    sin = np.sin(ang).astype(np.float32)
    cos_t = cos.reshape(16, 128, 64).transpose(1, 0, 2).reshape(128, 16 * 64)
    sin_t = np.stack([-sin, sin], axis=1).reshape(16, 128, 2, 64).transpose(1, 0, 2, 3).reshape(128, 16 * 128)
    lg = np.log(1.0 - np.power(2.0, -5.0 - np.arange(HEADS, dtype=np.float64)))
    idx = np.arange(128, dtype=np.float64)
    scale = 128.0 ** -0.5
    diff = idx[None, :] - idx[:, None]
    maskT = np.zeros((128, HEADS, 128), np.float64)
    for h in range(HEADS):
        maskT[:, h, :] = np.where(diff >= 0, np.exp(lg[h] * np.maximum(diff, 0.0)), 0.0) * scale
    kdec = np.zeros((128, HEADS, 128), np.float64)
    qdec = np.zeros((128, HEADS, 128), np.float64)
    for h in range(HEADS):
        kdec[:, h, :] = (np.exp(lg[h] * (127.0 - idx)) * scale)[:, None]
        qdec[:, h, :] = np.exp(lg[h] * (idx + 1.0))[:, None]
    return (np.ascontiguousarray(cos_t, np.float32), np.ascontiguousarray(sin_t, np.float32),
            maskT.reshape(128, 512).astype(np.float32), kdec.reshape(128, 512).astype(np.float32),
            qdec.reshape(128, 512).astype(np.float32))


_NC_CACHE = {}


def kernel(x, p, g_mix, w_in, conv_w, ret_gn, w_out, g_ffn, w_gate, w_up, w_down, g_ple,
           w_ple_gate, w_ple_proj, g_final):
    f = lambda a: np.ascontiguousarray(np.asarray(a, dtype=np.float32))
    x = f(x).reshape(NCORE, TOKC, D)
    p = f(p)[0].reshape(NCORE, TOKC, PLE)
    cos_t, sin_t, maskT, kdec, qdec = _tables()
    convw = np.ascontiguousarray(f(conv_w)[0].reshape(3, 4, 128).transpose(2, 1, 0).reshape(128, 12))
    ident = np.eye(128, dtype=np.float32).astype(ml_dtypes.bfloat16)
    shared = {
        "w_in": f(w_in)[0], "w_out": f(w_out)[0], "w_gate": f(w_gate)[0], "w_up": f(w_up)[0],
        "w_down": f(w_down)[0], "w_ple_gate": f(w_ple_gate)[0], "w_ple_proj": f(w_ple_proj)[0],
        "g_mix": f(g_mix)[0], "g_ffn": f(g_ffn)[0], "g_ple": f(g_ple)[0], "g_final": f(g_final),
        "ret_gn": f(ret_gn)[0], "convw": convw, "ident": ident, "cos_t": cos_t, "sin_t": sin_t,
        "maskT": maskT, "kdec": kdec, "qdec": qdec,
    }
    if "nc" not in _NC_CACHE:
        _NC_CACHE["nc"] = build_nc()
    nc = _NC_CACHE["nc"]
    in_maps = [dict(shared, x=x[i], p=p[i]) for i in range(NCORE)]
    res = run_bass_kernel_spmd(nc, in_maps, core_ids=list(range(NCORE)))
    out = np.stack([np.asarray(r["out"], dtype=np.float32) for r in res.results], axis=0)
    return out.reshape(16, SEQ, D)
```

```python
import numpy as np
import ml_dtypes
import concourse.bass as bass
import concourse.mybir as mybir
from concourse.bass_utils import run_bass_kernel_spmd

F32 = mybir.dt.float32
BF16 = mybir.dt.bfloat16
AF = mybir.ActivationFunctionType
ALU = mybir.AluOpType
AX = mybir.AxisListType

D = 1024
SEQ = 2048
NCORE = 8
TOKC = 4096
GT = 1024
NCH = 8
NG = TOKC // GT
INC = 3584
DFF = 2816
NFF = 22
PLE = 256
EPS = 1e-6
HEADS = 4
FFB = 4
ARENA_BYTES = 206 * 1024


class Op:
    __slots__ = ("eng", "fn", "deps", "is_dma", "key", "signal", "tick", "idx")


class Prog:
    def __init__(self):
        self.ops = []
        self.recs = {}
        self.dma_count = {}

    def _access(self, reg, idx, is_write, out):
        space, lo, hi = reg
        lst = self.recs.setdefault(space, [])
        me = self.ops_eng(idx)
        keep = []
        excl = space == "ps"
        for rec in lst:
            rlo, rhi, ridx, rw = rec
            if rhi <= lo or rlo >= hi:
                keep.append(rec)
                continue
            if ridx != idx and (is_write or rw or (excl and self.ops_eng(ridx) != me)):
                kind = "RAW" if (rw and not is_write) else "X"
                out.append((ridx, kind))
            if is_write and rlo >= lo and rhi <= hi:
                continue
            if (not is_write) and (not rw) and rlo == lo and rhi == hi and self.ops_eng(ridx) == me:
                continue
            keep.append(rec)
        if (not is_write) and space == "sb" and not any(rec[3] and rec[0] < hi and rec[1] > lo for rec in lst):
            print("WARNING: read of never-written region", reg, "op", idx, self._cur_eng)
        keep.append([lo, hi, idx, is_write])
        self.recs[space] = keep

    def ops_eng(self, idx):
        if idx == len(self.ops):
            return self._cur_eng
        o = self.ops[idx]
        return ("dma", o.idx) if o.is_dma else o.eng

    def add(self, eng, fn, r=(), w=(), dma_key=None):
        op = Op()
        op.eng = eng
        op.fn = fn
        op.is_dma = dma_key is not None
        op.key = dma_key
        op.signal = op.is_dma
        op.tick = 0
        op.idx = len(self.ops)
        self._cur_eng = ("dma", op.idx) if op.is_dma else eng
        raw = []
        for reg in r:
            self._access(reg, op.idx, False, raw)
        for reg in w:
            self._access(reg, op.idx, True, raw)
        deps = {}
        for (pidx, kind) in raw:
            p = self.ops[pidx]
            if p.is_dma:
                deps[pidx] = True
                continue
            if (not op.is_dma) and p.eng == eng:
                if eng == "pe":
                    continue
            deps[pidx] = True
        best = {}
        final = []
        for pidx in deps:
            p = self.ops[pidx]
            if p.is_dma:
                final.append(pidx)
            else:
                if p.eng not in best or best[p.eng] < pidx:
                    best[p.eng] = pidx
        final.extend(best.values())
        for pidx in final:
            self.ops[pidx].signal = True
        op.deps = final
        self.ops.append(op)
        return op

    def emit(self, nc, block, sems, dma_sems):
        cnt = {}
        for op in self.ops:
            if op.is_dma:
                c = self.dma_count.get(op.key, 0) + 16
                self.dma_count[op.key] = c
                op.tick = c
            elif op.signal:
                c = cnt.get(op.eng, 0) + 1
                cnt[op.eng] = c
                op.tick = c
        self.final_cnt = cnt
        ops = self.ops

        def run(engname, e):
            waited = {}
            for op in ops:
                if op.eng != engname:
                    continue
                for pidx in op.deps:
                    p = ops[pidx]
                    if p.is_dma:
                        sem = dma_sems[p.key]
                        sk = ("d", p.key)
                    else:
                        sem = sems[p.eng]
                        sk = ("e", p.eng)
                    if waited.get(sk, 0) >= p.tick:
                        continue
                    waited[sk] = p.tick
                    e.wait_ge(sem, p.tick)
                ins = op.fn(e)
                if op.is_dma:
                    ins.then_inc(dma_sems[op.key], 16)
                elif op.signal:
                    ins.then_inc(sems[op.eng], 1)
            if engname == "sp":
                for k, c in self.dma_count.items():
                    if waited.get(("d", k), 0) < c:
                        e.wait_ge(dma_sems[k], c)

        @block.sync
        def _(e):
            run("sp", e)

        @block.scalar
        def _(e):
            run("act", e)

        @block.gpsimd
        def _(e):
            run("pool", e)

        @block.tensor
        def _(e):
            run("pe", e)

        @block.vector
        def _(e):
            run("dve", e)


class Buf:
    def __init__(self, arena, lo, n):
        self.arena = arena
        self.lo = lo
        self.n = n

    def f32(self, pat=None, **kw):
        a = self.arena[:, self.lo // 4:(self.lo + self.n) // 4]
        return a.rearrange(pat, **kw) if pat else a

    def bf(self, pat=None, **kw):
        a = self.arena[:, self.lo // 4:(self.lo + self.n) // 4].bitcast(BF16)
        return a.rearrange(pat, **kw) if pat else a

    def r(self, lo=0, hi=None):
        return ("sb", self.lo + lo, self.lo + (self.n if hi is None else hi))


class Bump:
    def __init__(self, arena, lo, hi):
        self.arena, self.lo, self.hi, self.p = arena, lo, hi, lo

    def alloc(self, n):
        n = (n + 63) // 64 * 64
        b = Buf(self.arena, self.p, n)
        self.p += n
        assert self.p <= self.hi, ("arena overflow", self.p, self.hi)
        return b


class Ring:
    def __init__(self, arena, lo, size):
        self.arena, self.lo, self.size = arena, lo, size
        self.head = 0
        self.live = []

    def try_alloc(self, n):
        start = self.head if self.head + n <= self.size else 0
        if all(start + n <= o or start >= o + m for (o, m) in self.live):
            self.head = start + n
            self.live.append((start, n))
            return Buf(self.arena, self.lo + start, n)
        return None

    def free(self, buf):
        self.live.remove((buf.lo - self.lo, buf.n))


def build_nc(ngroups=NG, stop_phase=3):
    nc = bass.Bass("TRN2", target_bir_lowering=False)
    dt_in = lambda name, shape, dt=F32: nc.dram_tensor(name, shape, dt, kind="ExternalInput").ap()
    x_d = dt_in("x", [TOKC, D])
    p_d = dt_in("p", [TOKC, PLE])
    w_in_d = dt_in("w_in", [D, INC])
    w_out_d = dt_in("w_out", [D, D])
    w_gate_d = dt_in("w_gate", [D, DFF])
    w_up_d = dt_in("w_up", [D, DFF])
    w_down_d = dt_in("w_down", [DFF, D])
    w_pg_d = dt_in("w_ple_gate", [D, D])
    w_pp_d = dt_in("w_ple_proj", [PLE, D])
    g_mix_d = dt_in("g_mix", [D])
    g_ffn_d = dt_in("g_ffn", [D])
    g_ple_d = dt_in("g_ple", [D])
    g_fin_d = dt_in("g_final", [D])
    retgn_d = dt_in("ret_gn", [512])
    convw_d = dt_in("convw", [128, 12])
    ident_d = dt_in("ident", [128, 128], BF16)
    cos_d = dt_in("cos_t", [128, 16 * 64])
    sin_d = dt_in("sin_t", [128, 16 * 128])
    mask_d = dt_in("maskT", [128, 512])
    kdec_d = dt_in("kdec", [128, 512])
    qdec_d = dt_in("qdec", [128, 512])
    out_d = nc.dram_tensor("out", [TOKC, D], F32, kind="ExternalOutput").ap()

    log_gamma = np.log(1.0 - np.power(2.0, -5.0 - np.arange(HEADS, dtype=np.float64)))
    cdec = [float(np.exp(lg * 128.0)) for lg in log_gamma]

    P = Prog()
    from contextlib import ExitStack
    es = ExitStack()
    arena_t = es.enter_context(nc.sbuf_tensor("arena", [128, ARENA_BYTES // 4], F32))
    ps_t = es.enter_context(nc.psum_tensor("ps", [128, 8, 512], F32))
    arena = arena_t[:]
    pers = Bump(arena, 0, ARENA_BYTES)

    hbuf = [pers.alloc(4096) for _ in range(NCH)]
    cosb = pers.alloc(NCH * 64 * 4)
    sinb = pers.alloc(NCH * 128 * 4)
    maskb = pers.alloc(2048)
    kdecb = pers.alloc(2048)
    qdecb = pers.alloc(2048)
    retgnb = pers.alloc(2048)
    identb = pers.alloc(256)
    stateb = pers.alloc(2048)
    statebf = pers.alloc(1024)
    convwb = pers.alloc(64)
    halob = pers.alloc(64)
    statb = pers.alloc(256)
    epsb = pers.alloc(64)
    WORK_LO = pers.p
    WORK_SIZE = 78 * 1024
    RING_LO = WORK_LO + WORK_SIZE
    RING_SIZE = ARENA_BYTES - RING_LO
    ring = Ring(arena, RING_LO, RING_SIZE)
    ident = identb.bf()

    ps_ctr = [0]

    def nb():
        b = ps_ctr[0] % 8
        ps_ctr[0] += 1
        return b

    def psr(b):
        return ("ps", b, b + 1)

    def psf(b):
        return ps_t[:, b, :]

    def psb(b):
        return ps_t[:, b, :].bitcast(BF16)

    dma_keys = set()

    def dma(eng, out, in_, r, w, key):
        dma_keys.add(key)
        P.add(eng, lambda e: e.dma_start(out=out, in_=in_), r=r, w=w, dma_key=key)

    def mm(out, lhsT, rhs, start, stop, r, w):
        P.add("pe", lambda e: e.matmul(out, lhsT=lhsT, rhs=rhs, start=start, stop=stop), r=r, w=w)

    def tr(out, in_, r, w):
        P.add("pe", lambda e: e.transpose(out=out, in_=in_, identity=ident), r=list(r) + [identb.r()], w=w)

    def tt(out, in0, in1, op, r, w):
        P.add("dve", lambda e: e.tensor_tensor(out=out, in0=in0, in1=in1, op=op), r=r, w=w)

    def ptt(out, in0, in1, op, r, w):
        P.add("pool", lambda e: e.tensor_tensor(out=out, in0=in0, in1=in1, op=op), r=r, w=w)

    def stt(out, in0, scalar, in1, op0, op1, r, w):
        P.add("dve", lambda e: e.scalar_tensor_tensor(out=out, in0=in0, scalar=scalar, in1=in1, op0=op0, op1=op1), r=r, w=w)

    def ts(out, in0, s1, s2, op0, op1, r, w):
        if op1 is None:
            P.add("dve", lambda e: e.tensor_scalar(out=out, in0=in0, scalar1=s1, scalar2=None, op0=op0), r=r, w=w)
        else:
            P.add("dve", lambda e: e.tensor_scalar(out=out, in0=in0, scalar1=s1, scalar2=s2, op0=op0, op1=op1), r=r, w=w)

    def rsum(out, in_, r, w):
        P.add("dve", lambda e: e.reduce_sum(out=out, in_=in_, axis=AX.X), r=r, w=w)

    def mset(ap, val, w):
        P.add("dve", lambda e: e.memset(ap, val), r=[], w=w)

    def act(out, in_, func, r, w, accum_out=None):
        if accum_out is None:
            P.add("act", lambda e: e.activation(out=out, in_=in_, func=func), r=r, w=w)
        else:
            P.add("act", lambda e: e.activation(out=out, in_=in_, func=func, accum_out=accum_out), r=r, w=w)

    cidx = [0]

    def load_const(buf, src, bf=False):
        dst = buf.bf() if bf else buf.f32()
        dst = dst[:, 0:src.shape[1]]
        dma("sp", dst, src, [], [buf.r()], "c%d" % cidx[0])
        cidx[0] += 1

    P.add("dve", lambda e: e.memset(epsb.f32(), EPS), r=[], w=[epsb.r()])
    load_const(identb, ident_d, bf=True)
    load_const(maskb, mask_d)
    load_const(kdecb, kdec_d)
    load_const(qdecb, qdec_d)
    load_const(convwb, convw_d)
    load_const(retgnb, retgn_d.partition_broadcast(128))

    def wview(d, c0, c1):
        return d.rearrange("(kc p) n -> p kc n", p=128)[:, :, c0:c1]

    def piece_list(g):
        L = []
        for nm, c0 in (("q", 1536), ("k", 2048), ("v", 2560), ("g", 3072), ("cc", 512), ("cx", 1024), ("cb", 0)):
            L.append(((g, nm), wview(w_in_d, c0, c0 + 512), 8 * 512 * 2))
        for hf in range(2):
            L.append(((g, "wo%d" % hf), wview(w_out_d, hf * 512, hf * 512 + 512), 8 * 512 * 2))
        if stop_phase >= 2:
            f0 = 0
            bi = 0
            while f0 < NFF:
                nf = min(FFB, NFF - f0)
                L.append(((g, "fg%d" % bi), wview(w_gate_d, f0 * 128, (f0 + nf) * 128), 8 * nf * 128 * 2))
                L.append(((g, "fu%d" % bi), wview(w_up_d, f0 * 128, (f0 + nf) * 128), 8 * nf * 128 * 2))
                L.append(((g, "fd%d" % bi), w_down_d.rearrange("(f p) n -> p f n", p=128)[:, f0:f0 + nf, :], nf * 1024 * 2))
                f0 += nf
                bi += 1
        if stop_phase >= 3:
            for hf in range(2):
                L.append(((g, "pg%d" % hf), wview(w_pg_d, hf * 512, hf * 512 + 512), 8 * 512 * 2))
            L.append(((g, "pp"), w_pp_d.rearrange("(k p) n -> p k n", p=128), 2 * 1024 * 2))
        return L

    pending = []
    for g in range(ngroups):
        pending.extend(piece_list(g))
    pending.reverse()
    loaded = {}
    wsem_ctr = [0]
    NWSEM = 24

    def pump():
        while pending:
            name, src, nbytes = pending[-1]
            b = ring.try_alloc(nbytes)
            if b is None:
                return
            pending.pop()
            loaded[name] = b
            dst = b.bf("p (a n) -> p a n", a=src.shape[1])
            key = "w%d" % (wsem_ctr[0] % NWSEM)
            wsem_ctr[0] += 1
            dma("pool", dst, src, [], [b.r()], key)

    def W(g, nm):
        pump()
        assert (g, nm) in loaded, ("weight piece not resident (ring too small?)", g, nm, ring.live)
        return loaded[(g, nm)]

    def wfree(g, nm):
        ring.free(loaded.pop((g, nm)))
        pump()

    stat = statb.f32()

    def st(i, n=1):
        return stat[:, i:i + n]

    def str_(i, n=1):
        return statb.r(i * 4, (i + n) * 4)

    def rstd_from_ss(ss_i, rs_i, n, w=1):
        sq = st(rs_i, w)
        P.add("act", lambda e: e.activation(out=sq, in_=st(ss_i, w), func=AF.Sqrt, bias=epsb.f32()[:, 0:1], scale=1.0 / n),
              r=[str_(ss_i, w), epsb.r()], w=[str_(rs_i, w)])
        P.add("dve", lambda e: e.reciprocal(out=sq, in_=sq), r=[str_(rs_i, w)], w=[str_(rs_i, w)])

    def norm_to_bf16(c, gbuf, ubuf, slot):
        h = hbuf[c].f32()
        ss_i, rs_i = slot * 2, slot * 2 + 1
        act(ubuf.bf(), h, AF.Square, [hbuf[c].r()], [ubuf.r(), str_(ss_i)], accum_out=st(ss_i))
        rstd_from_ss(ss_i, rs_i, D)
        stt(ubuf.bf(), h, st(rs_i), gbuf.f32(), ALU.mult, ALU.mult, [hbuf[c].r(), str_(rs_i), gbuf.r()], [ubuf.r()])

    def transpose_to(ubuf, nk, dst_ap, dst_regs, eng="act"):
        b = nb()
        u = ubuf.bf()
        for k in range(nk):
            tr(psb(b)[:, k * 128:(k + 1) * 128], u[:, k * 128:(k + 1) * 128], [ubuf.r(k * 256, (k + 1) * 256)], [psr(b)])
        src = psb(b)[:, 0:nk * 128].rearrange("p (k n) -> p k n", k=nk)
        if eng == "act":
            act(dst_ap, src, AF.Copy, [psr(b)], dst_regs)
        else:
            P.add("dve", lambda e: e.tensor_copy(out=dst_ap, in_=src), r=[psr(b)], w=dst_regs)

    def load_g(buf, src, key):
        dma("sp", buf.f32(), src.partition_broadcast(128), [], [buf.r()], key)

    wk = Bump(arena, WORK_LO, WORK_LO + WORK_SIZE)
    gmixb = wk.alloc(4096)
    ub = [wk.alloc(2048) for _ in range(2)]
    uT = [wk.alloc(8192) for _ in range(2)]
    yT = [wk.alloc(8192) for _ in range(2)]
    ccb = wk.alloc(2048)
    zb = wk.alloc(514 * 4)
    cvb = wk.alloc(2048)
    TAIL_LIVE = wk.p - WORK_LO
    ropeA = [wk.alloc(2048) for _ in range(2)]
    ropeB = [wk.alloc(2048) for _ in range(2)]
    qksb2 = [wk.alloc(2048) for _ in range(2)]
    qkT = [wk.alloc(2048) for _ in range(2)]
    vsb = [wk.alloc(1024) for _ in range(2)]
    gateb = [wk.alloc(2048) for _ in range(2)]
    ksb2 = [wk.alloc(1024) for _ in range(2)]
    scsb = [wk.alloc(1024) for _ in range(2)]
    y1b = wk.alloc(2048)
    yb = wk.alloc(2048)
    yrb = wk.alloc(1024)
    wk2 = Bump(arena, WORK_LO + TAIL_LIVE, WORK_LO + WORK_SIZE)
    gffnb = wk2.alloc(4096)
    u2 = [wk2.alloc(2048) for _ in range(4)]
    uT2 = wk2.alloc(16384)
    wk2b = Bump(arena, WORK_LO, WORK_LO + TAIL_LIVE)
    sgf = [wk2b.alloc(2048) for _ in range(2)]
    aT = [wk2b.alloc(FFB * 1024) for _ in range(2)]
    u2v = uT2.bf("p (c k n) -> p c k n", c=8, k=8)

    wk = Bump(arena, WORK_LO + TAIL_LIVE, WORK_LO + WORK_SIZE)
    gpleb = wk.alloc(4096)
    gfinb = wk.alloc(4096)
    u3 = [wk.alloc(2048) for _ in range(2)]
    uT3 = [wk.alloc(2048) for _ in range(2)]
    pbf = [wk.alloc(512) for _ in range(2)]
    pT = [wk.alloc(512) for _ in range(2)]
    sig = [wk.alloc(2048) for _ in range(2)]
    tb = [wk.alloc(2048) for _ in range(2)]
    junkb = wk.alloc(2048)

    def S1(c, tok0_):
        dma("sp", hbuf[c].f32(), x_d[tok0_ + c * 128: tok0_ + (c + 1) * 128, :], [], [hbuf[c].r()], "h%d" % c)
        norm_to_bf16(c, gmixb, ub[c % 2], c % 2)

    def uT4(t_):
        return uT[t_ % 2].bf("p (c k n) -> p c k n", c=4, k=8)

    def yT4(t_):
        return yT[t_ % 2].bf("p (c m n) -> p c m n", c=4, m=8)

    def S2(c):
        t_, c4 = divmod(c, 4)
        transpose_to(ub[c % 2], 8, uT4(t_)[:, c4], [uT[t_ % 2].r(c4 * 2048, (c4 + 1) * 2048)])

    def group_prologue(gn):
        pc0 = (gn % 2) * NCH
        dma("sp", cosb.f32(), cos_d[:, pc0 * 64:(pc0 + NCH) * 64], [], [cosb.r()], "tabc")
        dma("sp", sinb.f32(), sin_d[:, pc0 * 128:(pc0 + NCH) * 128], [], [sinb.r()], "tabs")
        if gn % 2 == 0:
            mset(stateb.f32(), 0.0, [stateb.r()])
            mset(statebf.bf(), 0.0, [statebf.r()])
            mset(halob.f32(), 0.0, [halob.r()])
        load_g(gmixb, g_mix_d, "g0")
        S1(0, gn * GT)
        S1(1, gn * GT)

    group_prologue(0)
    S2(0)
    S1(2, 0)
    S2(1)

    for g in range(ngroups):
        seq_first = (g % 2 == 0)
        tok0 = g * GT
        pos_chunk0 = (g % 2) * NCH

        def mm_tok(c, wbuf, bank):
            t_, c4 = divmod(c, 4)
            wv = wbuf.bf("p (k n) -> p k n", k=8)
            ur = uT[t_ % 2].r(c4 * 2048, (c4 + 1) * 2048)
            for kc in range(8):
                mm(psf(bank), uT4(t_)[:, c4, kc, :], wv[:, kc, :], kc == 0, kc == 7, [ur, wbuf.r()], [psr(bank)])

        def S3(c):
            bq, bk, bv, bg = nb(), nb(), nb(), nb()
            qksb, ksb = qksb2[c % 2], ksb2[c % 2]
            mm_tok(c, W(g, "q"), bq)
            mm_tok(c, W(g, "k"), bk)
            mm_tok(c, W(g, "v"), bv)
            mm_tok(c, W(g, "g"), bg)
            cosv = cosb.f32("p (c d) -> p c d", c=NCH)[:, c, :].unsqueeze(1).unsqueeze(1).broadcast_to([128, 4, 2, 64])
            sinv = sinb.f32("p (c t d) -> p c t d", c=NCH, t=2)
            for bi_, (bank, off) in enumerate(((bq, 0), (bk, 512))):
                rA, rB = ropeA[bi_], ropeB[bi_]
                a4 = rA.f32("p (h t d) -> p h t d", h=4, t=2)
                b4 = rB.f32("p (h t d) -> p h t d", h=4, t=2)
                x4 = psf(bank).rearrange("p (h t d) -> p h t d", h=4, t=2)
                tt(a4, x4, cosv, ALU.mult, [psr(bank), cosb.r()], [rA.r()])
                for t2 in range(2):
                    sv = sinv[:, c, t2, :].unsqueeze(1).broadcast_to([128, 4, 64])
                    tt(b4[:, :, t2, :], x4[:, :, 1 - t2, :], sv, ALU.mult, [psr(bank), sinb.r()], [rB.r()])
                ptt(qksb.bf()[:, off:off + 512], rA.f32(), rB.f32(), ALU.add, [rA.r(), rB.r()],
                    [qksb.r(off * 2, (off + 512) * 2)])
            ptt(ksb.bf(), qksb.bf()[:, 512:1024], kdecb.f32(), ALU.mult, [qksb.r(1024, 2048), kdecb.r()], [ksb.r()])
            act(vsb[c % 2].bf(), psf(bv), AF.Copy, [psr(bv)], [vsb[c % 2].r()])
            act(gateb[c % 2].f32(), psf(bg), AF.Silu, [psr(bg)], [gateb[c % 2].r()])
            ptt(gateb[c % 2].f32(), gateb[c % 2].f32(), retgnb.f32(), ALU.mult, [gateb[c % 2].r(), retgnb.r()], [gateb[c % 2].r()])

        def S4(c):
            transpose_to(qksb2[c % 2], 8, qkT[c % 2].bf("p (k n) -> p k n", k=8), [qkT[c % 2].r()])

        def S5(c):
            b = nb()
            qk = qkT[c % 2].bf("p (k n) -> p k n", k=8)
            for h in range(HEADS):
                mm(psf(b)[:, h * 128:(h + 1) * 128], qk[:, 4 + h, :], qk[:, h, :], True, True, [qkT[c % 2].r()], [psr(b)])
            tt(scsb[c % 2].bf(), psf(b), maskb.f32(), ALU.mult, [psr(b), maskb.r()], [scsb[c % 2].r()])

        def S6(c):
            boi, boc, bkv = nb(), nb(), nb()
            ksb = ksb2[c % 2]
            qk = qkT[c % 2].bf("p (k n) -> p k n", k=8)
            sc = scsb[c % 2].bf("p (h n) -> p h n", h=4)
            v = vsb[c % 2].bf()
            for h in range(HEADS):
                sl = slice(h * 128, (h + 1) * 128)
                mm(psf(boi)[:, sl], sc[:, h, :], v[:, sl], True, True, [scsb[c % 2].r(), vsb[c % 2].r()], [psr(boi)])
            for h in range(HEADS):
                sl = slice(h * 128, (h + 1) * 128)
                mm(psf(boc)[:, sl], qk[:, h, :], statebf.bf()[:, sl], True, True, [qkT[c % 2].r(), statebf.r()], [psr(boc)])
            for h in range(HEADS):
                sl = slice(h * 128, (h + 1) * 128)
                mm(psf(bkv)[:, sl], ksb.bf()[:, sl], v[:, sl], True, True, [ksb.r(), vsb[c % 2].r()], [psr(bkv)])
            tt(y1b.f32(), psf(boc), qdecb.f32(), ALU.mult, [psr(boc), qdecb.r()], [y1b.r()])
            tt(yb.f32(), psf(boi), y1b.f32(), ALU.add, [psr(boi), y1b.r()], [yb.r()])
            for h in range(HEADS):
                sl = slice(h * 128, (h + 1) * 128)
                sr = stateb.r(h * 512, (h + 1) * 512)
                stt(stateb.f32()[:, sl], stateb.f32()[:, sl], cdec[h], psf(bkv)[:, sl], ALU.mult, ALU.add, [sr, psr(bkv)], [sr])
            act(statebf.bf(), stateb.f32(), AF.Copy, [stateb.r()], [statebf.r()])
            y3 = yb.f32("p (h d) -> p h d", h=4)
            B0 = 16
            act(y1b.f32(), yb.f32(), AF.Square, [yb.r()], [y1b.r()])
            rsum(st(B0, 4), y3, [yb.r()], [str_(B0, 4)])
            rsum(st(B0 + 4, 4), y1b.f32("p (h d) -> p h d", h=4), [y1b.r()], [str_(B0 + 4, 4)])
            ts(st(B0 + 8, 4), st(B0, 4), 1.0 / 128, None, ALU.mult, None, [str_(B0, 4)], [str_(B0 + 8, 4)])
            tt(st(B0 + 12, 4), st(B0 + 8, 4), st(B0 + 8, 4), ALU.mult, [str_(B0 + 8, 4)], [str_(B0 + 12, 4)])
            stt(st(B0 + 16, 4), st(B0 + 4, 4), 1.0 / 128, st(B0 + 12, 4), ALU.mult, ALU.subtract,
                [str_(B0 + 4, 4), str_(B0 + 12, 4)], [str_(B0 + 16, 4)])
            rstd_from_ss(B0 + 16, B0 + 20, 1.0, w=4)
            stt(st(B0 + 24, 4), st(B0 + 8, 4), -1.0, st(B0 + 20, 4), ALU.mult, ALU.mult,
                [str_(B0 + 8, 4), str_(B0 + 20, 4)], [str_(B0 + 24, 4)])
            for h in range(HEADS):
                yh = y3[:, h, :]
                sc_ap = st(B0 + 20 + h)
                bi_ap = st(B0 + 24 + h)
                P.add("act", lambda e, yh=yh, sc_ap=sc_ap, bi_ap=bi_ap: e.activation(out=yh, in_=yh, func=AF.Identity, bias=bi_ap, scale=sc_ap),
                      r=[yb.r(h * 512, (h + 1) * 512), str_(B0 + 20 + h), str_(B0 + 24 + h)], w=[yb.r(h * 512, (h + 1) * 512)])
            tt(yrb.bf(), yb.f32(), gateb[c % 2].f32(), ALU.mult, [yb.r(), gateb[c % 2].r()], [yrb.r()])

        def S7(c):
            t_, c4 = divmod(c, 4)
            transpose_to(yrb, 4, yT4(t_)[:, c4, 4:8, :], [yT[t_ % 2].r(c4 * 2048 + 1024, (c4 + 1) * 2048)])

        def Vconv(t_, j):
            u4 = uT4(t_)
            ur = uT[t_ % 2].r()

            def mm_feat(wbuf, bank):
                wv = wbuf.bf("p (k n) -> p k n", k=8)
                for kc in range(8):
                    mm(psf(bank).rearrange("p (c n) -> p c n", c=4), wv[:, kc, j * 128:(j + 1) * 128], u4[:, :, kc, :],
                       kc == 0, kc == 7, [ur, wbuf.r()], [psr(bank)])

            bcc, bcx, bcb = nb(), nb(), nb()
            z = zb.f32()
            cv = cvb.f32()
            mm_feat(W(g, "cc"), bcc)
            act(ccb.f32(), psf(bcc), AF.Copy, [psr(bcc)], [ccb.r()])
            mm_feat(W(g, "cx"), bcx)
            hal = halob.f32()[:, j * 2:(j + 1) * 2]
            halr = halob.r(j * 8, (j + 1) * 8)
            act(z[:, 0:2], hal, AF.Copy, [halr], [zb.r(0, 8)])
            tt(z[:, 2:514], psf(bcx), ccb.f32(), ALU.mult, [psr(bcx), ccb.r()], [zb.r(8, 514 * 4)])
            act(hal, z[:, 512:514], AF.Copy, [zb.r(512 * 4, 514 * 4)], [halr])
            cw = convwb.f32()
            ts(cv, z[:, 2:514], cw[:, j * 3 + 2:j * 3 + 3], None, ALU.mult, None, [zb.r(), convwb.r()], [cvb.r()])
            stt(cv, z[:, 1:513], cw[:, j * 3 + 1:j * 3 + 2], cv, ALU.mult, ALU.add, [zb.r(), convwb.r(), cvb.r()], [cvb.r()])
            stt(cv, z[:, 0:512], cw[:, j * 3:j * 3 + 1], cv, ALU.mult, ALU.add, [zb.r(), convwb.r(), cvb.r()], [cvb.r()])
            mm_feat(W(g, "cb"), bcb)
            y4 = yT4(t_)[:, :, j, :]
            yregs = [yT[t_ % 2].r(c4 * 2048 + j * 256, c4 * 2048 + (j + 1) * 256) for c4 in range(4)]
            tt(y4, psf(bcb).rearrange("p (c n) -> p c n", c=4), cvb.f32("p (c n) -> p c n", c=4), ALU.mult, [psr(bcb), cvb.r()], yregs)

        def h_add(c, hf, src_ap, src_reg):
            hs = hbuf[c].f32()[:, hf * 512:(hf + 1) * 512]
            hr = hbuf[c].r(hf * 2048, (hf + 1) * 2048)
            tt(hs, hs, src_ap, ALU.add, [hr, src_reg], [hr])

        def S8(c):
            t_, c4 = divmod(c, 4)
            y4 = yT4(t_)
            yr_ = yT[t_ % 2].r(c4 * 2048, (c4 + 1) * 2048)
            for hf in range(2):
                b = nb()
                wb_ = W(g, "wo%d" % hf)
                wv = wb_.bf("p (k n) -> p k n", k=8)
                for m in range(8):
                    mm(psf(b), y4[:, c4, m, :], wv[:, m, :], m == 0, m == 7, [yr_, wb_.r()], [psr(b)])
                h_add(c, hf, psf(b), psr(b))

        def N2a(c):
            norm_to_bf16(c, gffnb, u2[c % 4], 2 + c % 2)

        def N2b(c):
            transpose_to(u2[c % 4], 8, u2v[:, c], [uT2.r(c * 2048, (c + 1) * 2048)])

        S3(0)
        for c in range(NCH):
            if c + 3 < NCH:
                S1(c + 3, tok0)
            if c + 1 < NCH:
                S3(c + 1)
            if c + 2 < NCH:
                S2(c + 2)
            if c in (4, 5):
                Vconv(0, (c - 4) * 2)
            if c in (6, 7):
                S8((c - 6) * 2)
            S4(c)
            if c >= 1:
                S7(c - 1)
            S5(c)
            if c in (4, 5):
                Vconv(0, (c - 4) * 2 + 1)
            if c in (6, 7):
                S8((c - 6) * 2 + 1)
            S6(c)
        S7(NCH - 1)
        for nm in ("q", "k", "v", "g"):
            wfree(g, nm)
        p2 = stop_phase >= 2
        if p2:
            load_g(gffnb, g_ffn_d, "g1")
            N2a(0)
            N2a(1)
        Vconv(1, 0)
        if p2:
            N2b(0)
            N2a(2)
        Vconv(1, 1)
        if p2:
            N2b(1)
            N2a(3)
        Vconv(1, 2)
        if p2:
            N2b(2)
        Vconv(1, 3)
        if p2:
            N2b(3)
        for nm in ("cc", "cx", "cb"):
            wfree(g, nm)
        for c in range(4, 8):
            S8(c)
            if p2:
                N2a(c)
        wfree(g, "wo0")
        wfree(g, "wo1")

        if stop_phase >= 2:
            nblk = (NFF + FFB - 1) // FFB
            units = [(bi, t_) for bi in range(nblk) for t_ in range(2)]

            def ffn_up(ui):
                bi, t_ = units[ui]
                nf = min(FFB, NFF - bi * FFB)
                wg_, wu_ = W(g, "fg%d" % bi), W(g, "fu%d" % bi)
                wgv = wg_.bf("p (k n) -> p k n", k=8)
                wuv = wu_.bf("p (k n) -> p k n", k=8)
                a = aT[ui % 2]
                ur = uT2.r(t_ * 8192, (t_ + 1) * 8192)
                for f in range(nf):
                    bg_, bu_ = nb(), nb()
                    for (wv, wbuf, bank) in ((wgv, wg_, bg_), (wuv, wu_, bu_)):
                        for kc in range(8):
                            mm(psf(bank).rearrange("p (c n) -> p c n", c=4), wv[:, kc, f * 128:(f + 1) * 128],
                               u2v[:, t_ * 4:(t_ + 1) * 4, kc, :], kc == 0, kc == 7, [ur, wbuf.r()], [psr(bank)])
                    s_ = sgf[f % 2]
                    act(s_.f32(), psf(bg_), AF.Silu, [psr(bg_)], [s_.r()])
                    tt(a.bf()[:, f * 512:(f + 1) * 512], psf(bu_), s_.f32(), ALU.mult, [psr(bu_), s_.r()], [a.r(f * 1024, (f + 1) * 1024)])

            def ffn_down(ui):
                bi, t_ = units[ui]
                nf = min(FFB, NFF - bi * FFB)
                wd_ = W(g, "fd%d" % bi)
                wdv = wd_.bf("p (f n) -> p f n", f=nf)
                a = aT[ui % 2]
                av = a.bf("p (f n) -> p f n", f=FFB)
                for c4 in range(4):
                    c = t_ * 4 + c4
                    for hf in range(2):
                        b = nb()
                        for f in range(nf):
                            mm(psf(b), av[:, f, c4 * 128:(c4 + 1) * 128], wdv[:, f, hf * 512:(hf + 1) * 512], f == 0, f == nf - 1,
                               [a.r(), wd_.r()], [psr(b)])
                        h_add(c, hf, psf(b), psr(b))
                if t_ == 1:
                    wfree(g, "fg%d" % bi)
                    wfree(g, "fu%d" % bi)
                    wfree(g, "fd%d" % bi)

            for ui in range(len(units)):
                ffn_up(ui)
                if ui == 0:
                    for c in range(4, 8):
                        N2b(c)
                if ui >= 1:
                    ffn_down(ui - 1)
            ffn_down(len(units) - 1)

        SS3, RS3, SSF, RSF = 32, 40, 48, 56

        def N3a(c):
            dma("pool", pbf[c % 2].bf(), p_d[tok0 + c * 128: tok0 + (c + 1) * 128, :], [], [pbf[c % 2].r()], "p%d" % (c % 2))
            h = hbuf[c].f32()
            stt(u3[c % 2].bf(), h, st(RS3 + c), gpleb.f32(), ALU.mult, ALU.mult, [hbuf[c].r(), str_(RS3 + c), gpleb.r()], [u3[c % 2].r()])

        def N3b(c):
            transpose_to(u3[c % 2], 8, uT3[c % 2].bf("p (k n) -> p k n", k=8), [uT3[c % 2].r()], eng="act")
            transpose_to(pbf[c % 2], 2, pT[c % 2].bf("p (k n) -> p k n", k=2), [pT[c % 2].r()], eng="dve")

        def G3(c):
            ut = uT3[c % 2].bf("p (k n) -> p k n", k=8)
            pt = pT[c % 2].bf("p (k n) -> p k n", k=2)
            wpp_ = W(g, "pp")
            wppv = wpp_.bf("p (k n) -> p k n", k=2)
            for hf in range(2):
                bg_, bp_ = nb(), nb()
                wg_ = W(g, "pg%d" % hf)
                wgv = wg_.bf("p (k n) -> p k n", k=8)
                for kc in range(8):
                    mm(psf(bg_), ut[:, kc, :], wgv[:, kc, :], kc == 0, kc == 7, [uT3[c % 2].r(), wg_.r()], [psr(bg_)])
                for kc in range(2):
                    mm(psf(bp_), pt[:, kc, :], wppv[:, kc, hf * 512:(hf + 1) * 512], kc == 0, kc == 1, [pT[c % 2].r(), wpp_.r()], [psr(bp_)])
                s_ = sig[hf]
                t2 = tb[hf]
                act(s_.f32(), psf(bg_), AF.Sigmoid, [psr(bg_)], [s_.r()])
                tt(t2.f32(), psf(bp_), s_.f32(), ALU.mult, [psr(bp_), s_.r()], [t2.r()])
                hs = hbuf[c].f32()[:, hf * 512:(hf + 1) * 512]
                hr = hbuf[c].r(hf * 2048, (hf + 1) * 2048)
                ptt(hs, hs, t2.f32(), ALU.add, [hr, t2.r()], [hr])

        def sq_accum(c, col):
            act(junkb.bf(), hbuf[c].f32(), AF.Square, [hbuf[c].r()], [junkb.r(), str_(col)], accum_out=st(col))

        def F3_tile(t_):
            cs = range(t_ * 4, t_ * 4 + 4)
            for c in cs:
                sq_accum(c, SSF + c)
            rstd_from_ss(SSF + t_ * 4, RSF + t_ * 4, D, w=4)
            for c in cs:
                h = hbuf[c].f32()
                stt(h, h, st(RSF + c), gfinb.f32(), ALU.mult, ALU.mult, [hbuf[c].r(), str_(RSF + c), gfinb.r()], [hbuf[c].r()])
                dma("sp", out_d[tok0 + c * 128: tok0 + (c + 1) * 128, :], h, [hbuf[c].r()], [], "h%d" % c)

        if stop_phase >= 3:
            load_g(gpleb, g_ple_d, "g2")
            load_g(gfinb, g_fin_d, "g3")
            for c in range(NCH):
                sq_accum(c, SS3 + c)
            rstd_from_ss(SS3, RS3, D, w=8)
            N3a(0)
            for c in range(NCH):
                if c + 1 < NCH:
                    N3a(c + 1)
                N3b(c)
                if c >= 1:
                    G3(c - 1)
                if c == 4:
                    F3_tile(0)
                    if g + 1 < ngroups:
                        group_prologue(g + 1)
            G3(NCH - 1)
            if g + 1 < ngroups:
                S2(0)
                S1(2, (g + 1) * GT)
                S2(1)
            F3_tile(1)
            wfree(g, "pg0")
            wfree(g, "pg1")
            wfree(g, "pp")
        else:
            for c in range(NCH):
                dma("sp", out_d[tok0 + c * 128: tok0 + (c + 1) * 128, :], hbuf[c].f32(), [hbuf[c].r()], [], "h%d" % c)
            if g + 1 < ngroups:
                group_prologue(g + 1)
                S2(0)
                S1(2, (g + 1) * GT)
                S2(1)

    assert not pending and not loaded, (len(pending), list(loaded))

    sem_names = ["sp", "act", "pool", "pe", "dve"]
    sems = {n: es.enter_context(nc.semaphore("s_" + n)) for n in sem_names}
    dma_sems = {k: es.enter_context(nc.semaphore("d_" + k)) for k in sorted(dma_keys)}
    block = es.enter_context(nc.Block())
    P.emit(nc, block, sems, dma_sems)
    es.close()
    nc._prog_stats = dict(nops=len(P.ops), ticks=P.final_cnt)
    return nc


def _tables():
    half = 64
    inv_freq = (10000.0 ** (-np.arange(half, dtype=np.float32) / half)).astype(np.float32)
    pos = np.arange(SEQ, dtype=np.float32)
    ang = (pos[:, None] * inv_freq[None, :]).astype(np.float32)
    cos = np.cos(ang).astype(np.float32)
    sin = np.sin(ang).astype(np.float32)
    cos_t = cos.reshape(16, 128, 64).transpose(1, 0, 2).reshape(128, 16 * 64)
    sin_t = np.stack([-sin, sin], axis=1).reshape(16, 128, 2, 64).transpose(1, 0, 2, 3).reshape(128, 16 * 128)
    lg = np.log(1.0 - np.power(2.0, -5.0 - np.arange(HEADS, dtype=np.float64)))
    idx = np.arange(128, dtype=np.float64)
    scale = 128.0 ** -0.5
    diff = idx[None, :] - idx[:, None]
    maskT = np.zeros((128, HEADS, 128), np.float64)
    for h in range(HEADS):
        maskT[:, h, :] = np.where(diff >= 0, np.exp(lg[h] * np.maximum(diff, 0.0)), 0.0) * scale
    kdec = np.zeros((128, HEADS, 128), np.float64)
    qdec = np.zeros((128, HEADS, 128), np.float64)
    for h in range(HEADS):
        kdec[:, h, :] = (np.exp(lg[h] * (127.0 - idx)) * scale)[:, None]
        qdec[:, h, :] = np.exp(lg[h] * (idx + 1.0))[:, None]
    return (np.ascontiguousarray(cos_t, np.float32), np.ascontiguousarray(sin_t, np.float32),
            maskT.reshape(128, 512).astype(np.float32), kdec.reshape(128, 512).astype(np.float32),
            qdec.reshape(128, 512).astype(np.float32))


_NC_CACHE = {}


def kernel(x, p, g_mix, w_in, conv_w, ret_gn, w_out, g_ffn, w_gate, w_up, w_down, g_ple,
           w_ple_gate, w_ple_proj, g_final):
    f = lambda a: np.ascontiguousarray(np.asarray(a, dtype=np.float32))
    x = f(x).reshape(NCORE, TOKC, D)
    p = f(p)[0].reshape(NCORE, TOKC, PLE)
    cos_t, sin_t, maskT, kdec, qdec = _tables()
    convw = np.ascontiguousarray(f(conv_w)[0].reshape(3, 4, 128).transpose(2, 1, 0).reshape(128, 12))
    ident = np.eye(128, dtype=np.float32).astype(ml_dtypes.bfloat16)
    shared = {
        "w_in": f(w_in)[0], "w_out": f(w_out)[0], "w_gate": f(w_gate)[0], "w_up": f(w_up)[0],
        "w_down": f(w_down)[0], "w_ple_gate": f(w_ple_gate)[0], "w_ple_proj": f(w_ple_proj)[0],
        "g_mix": f(g_mix)[0], "g_ffn": f(g_ffn)[0], "g_ple": f(g_ple)[0], "g_final": f(g_final),
        "ret_gn": f(ret_gn)[0], "convw": convw, "ident": ident, "cos_t": cos_t, "sin_t": sin_t,
        "maskT": maskT, "kdec": kdec, "qdec": qdec,
    }
    if "nc" not in _NC_CACHE:
        _NC_CACHE["nc"] = build_nc()
    nc = _NC_CACHE["nc"]
    in_maps = [dict(shared, x=x[i], p=p[i]) for i in range(NCORE)]
    res = run_bass_kernel_spmd(nc, in_maps, core_ids=list(range(NCORE)))
    out = np.stack([np.asarray(r["out"], dtype=np.float32) for r in res.results], axis=0)
    return out.reshape(16, SEQ, D)
```
